# Optimizing a Trainium2 kernel written in Bass

```python
import math
import jax, jax.numpy as jnp
from jax import lax
import numpy as np

D_MODEL = 2048
BATCH = 2
SEQ = 16384
DEPTH = 1

HEAD_DIM = 128
DILATED_GROUPS = ((128, 1), (512, 4), (2048, 16))
N_GROUPS = len(DILATED_GROUPS)
HEADS_PER_GROUP = 4
N_ATTN_HEADS = N_GROUPS * HEADS_PER_GROUP
ATTN_WIDTH = N_ATTN_HEADS * HEAD_DIM
ATTN_OUT_WIDTH = HEADS_PER_GROUP * HEAD_DIM
CONV_DIM = D_MODEL
CONV_WIDTH = 3
D_FF = 5632
BLOCK = 128
RMS_EPS = 1e-6
NEG_INF = -1e30
ALIBI_SLOPES = tuple(2.0 ** (-8.0 * (i + 1) / N_ATTN_HEADS) for i in range(N_ATTN_HEADS))
IN_SIZES = (ATTN_WIDTH, ATTN_WIDTH, ATTN_WIDTH,
            CONV_DIM, CONV_DIM, CONV_DIM,
            D_MODEL, D_MODEL)
W_IN_COLS = sum(IN_SIZES)
SPLIT_POINTS = tuple(int(v) for v in np.cumsum(IN_SIZES)[:-1])

kernel_name = "hybrid_dilated_attn_shortconv_macaron"


def rms_norm(x, gain):
    xf = x.astype(jnp.float32)
    y = xf * lax.rsqrt(jnp.mean(xf * xf, axis=-1, keepdims=True) + RMS_EPS)
    return (y * gain.astype(jnp.float32)).astype(x.dtype)


def swiglu(x, w_gate, w_up, w_down):
    return (jax.nn.silu(x @ w_gate) * (x @ w_up)) @ w_down


def dilated_window_attention(q, k, v, slopes, window, dilation):
    b, s, h, dh = q.shape
    L = s // dilation
    w = window // dilation
    nb = -(-L // BLOCK)
    Lp = nb * BLOCK

    def to_sub(t):
        t = t.reshape(b, L, dilation, h, dh).transpose(0, 2, 3, 1, 4)
        return jnp.pad(t, ((0, 0), (0, 0), (0, 0), (0, Lp - L), (0, 0)))

    def band(t):
        t = jnp.pad(t, ((0, 0), (0, 0), (0, 0), (BLOCK, 0), (0, 0)))
        t = t.reshape(b, dilation, h, nb + 1, BLOCK, dh)
        return jnp.concatenate([t[:, :, :, :-1], t[:, :, :, 1:]], axis=4)

    qs, ks, vs = to_sub(q), to_sub(k), to_sub(v)
    qb = qs.reshape(b, dilation, h, nb, BLOCK, dh)
    kb, vb = band(ks), band(vs)

    scores = jnp.einsum('brhnqc,brhnkc->brhnqk', qb, kb).astype(jnp.float32)
    qi = jnp.arange(BLOCK)[:, None]
    ki = jnp.arange(2 * BLOCK)[None, :]
    dist = BLOCK + qi - ki
    key_pos = jnp.arange(nb)[:, None, None] * BLOCK + ki[None] - BLOCK
    valid = (dist >= 0) & (dist <= w) & (key_pos >= 0)
    slopes_arr = jnp.asarray(slopes, jnp.float32)
    alibi = -slopes_arr[:, None, None, None] * (dilation * dist).astype(jnp.float32)
    scores = jnp.where(valid, scores + alibi, NEG_INF)

    m = jnp.max(scores, axis=-1, keepdims=True)
    p = jnp.exp(scores - m)
    l = jnp.sum(p, axis=-1, keepdims=True)
    out = jnp.einsum('brhnqk,brhnkc->brhnqc', (p / l).astype(v.dtype), vb)
    lse = (m + jnp.log(l))[..., 0]

    out = out.reshape(b, dilation, h, Lp, dh)[:, :, :, :L]
    out = out.transpose(0, 3, 1, 2, 4).reshape(b, s, h, dh)
    lse = lse.reshape(b, dilation, h, Lp)[..., :L]
    lse = lse.transpose(0, 3, 1, 2).reshape(b, s, h)
    return out, lse


def causal_depthwise_conv(z, w):
    c = z.shape[-1]
    return lax.conv_general_dilated(
        z, w.reshape(CONV_WIDTH, 1, c).astype(z.dtype),
        window_strides=(1,), padding=((CONV_WIDTH - 1, 0),),
        dimension_numbers=('NWC', 'WIO', 'NWC'), feature_group_count=c)


def setup_inputs(seed: int = 0) -> dict:
    key = jax.random.key(seed)
    ks = jax.random.split(key, 20)
    f32 = jnp.float32

    def dense(k, fan_in, fan_out, scale=1.0):
        return jax.random.normal(k, (DEPTH, fan_in, fan_out), f32) * (scale * fan_in ** -0.5)

    def gain(k, shape):
        return 1.0 + 0.02 * jax.random.normal(k, (DEPTH,) + shape, f32)

    return {
        "x": jax.random.normal(ks[0], (BATCH, SEQ, D_MODEL), f32),
        "ffn1_norm": gain(ks[1], (D_MODEL,)),
        "ffn1_w_gate": dense(ks[2], D_MODEL, D_FF),
        "ffn1_w_up": dense(ks[3], D_MODEL, D_FF),
        "ffn1_w_down": dense(ks[4], D_FF, D_MODEL),
        "mix_norm": gain(ks[5], (D_MODEL,)),
        "w_in": dense(ks[6], D_MODEL, W_IN_COLS),
        "q_norm": gain(ks[7], (N_GROUPS, HEADS_PER_GROUP, HEAD_DIM)),
        "k_norm": gain(ks[8], (N_GROUPS, HEADS_PER_GROUP, HEAD_DIM)),
        "conv_w": jax.random.normal(ks[9], (DEPTH, CONV_WIDTH, CONV_DIM), f32) * CONV_WIDTH ** -0.5,
        "w_attn_out": dense(ks[10], ATTN_OUT_WIDTH, D_MODEL),
        "w_conv_out": dense(ks[11], CONV_DIM, D_MODEL),
        "w_o": dense(ks[12], D_MODEL, D_MODEL),
        "ffn2_norm": gain(ks[13], (D_MODEL,)),
        "ffn2_w_gate": dense(ks[14], D_MODEL, D_FF),
        "ffn2_w_up": dense(ks[15], D_MODEL, D_FF),
        "ffn2_w_down": dense(ks[16], D_FF, D_MODEL),
    }


def reference(x, ffn1_norm, ffn1_w_gate, ffn1_w_up, ffn1_w_down, mix_norm, w_in,
              q_norm, k_norm, conv_w, w_attn_out, w_conv_out, w_o,
              ffn2_norm, ffn2_w_gate, ffn2_w_up, ffn2_w_down):
    b, s, _ = x.shape
    for layer in range(DEPTH):
        x = x + 0.5 * swiglu(rms_norm(x, ffn1_norm[layer]),
                             ffn1_w_gate[layer], ffn1_w_up[layer], ffn1_w_down[layer])

        h = rms_norm(x, mix_norm[layer])
        proj = h @ w_in[layer]
        q, k, v, u, gate_b, gate_c, g_attn, g_conv = jnp.split(proj, SPLIT_POINTS, axis=-1)

        hshape = (b, s, N_GROUPS, HEADS_PER_GROUP, HEAD_DIM)
        q = rms_norm(q.reshape(hshape), q_norm[layer]) * (HEAD_DIM ** -0.5)
        k = rms_norm(k.reshape(hshape), k_norm[layer])
        v = v.reshape(hshape)
        outs, lses = [], []
        for g, (window, dilation) in enumerate(DILATED_GROUPS):
            slopes = ALIBI_SLOPES[g * HEADS_PER_GROUP:(g + 1) * HEADS_PER_GROUP]
            o_g, lse_g = dilated_window_attention(q[:, :, g], k[:, :, g], v[:, :, g],
                                                  slopes, window, dilation)
            outs.append(o_g)
            lses.append(lse_g)
        outs = jnp.stack(outs, axis=0)
        alpha = jax.nn.softmax(jnp.stack(lses, axis=0), axis=0)
        attn = jnp.sum(alpha[..., None].astype(outs.dtype) * outs, axis=0)
        branch_a = attn.reshape(b, s, ATTN_OUT_WIDTH) @ w_attn_out[layer]

        y = gate_b * causal_depthwise_conv(gate_c * u, conv_w[layer])
        branch_b = y @ w_conv_out[layer]

        merged = jax.nn.sigmoid(g_attn) * branch_a + jax.nn.sigmoid(g_conv) * branch_b
        x = x + merged @ w_o[layer]

        x = x + 0.5 * swiglu(rms_norm(x, ffn2_norm[layer]),
                             ffn2_w_gate[layer], ffn2_w_up[layer], ffn2_w_down[layer])
    return x
```

```python
import numpy as np
from contextlib import ExitStack
import concourse.bass as bass
import concourse.mybir as mybir
from concourse.bass_utils import run_bass_kernel_spmd

F32 = mybir.dt.float32
BF16 = mybir.dt.bfloat16
AF = mybir.ActivationFunctionType
ALU = mybir.AluOpType

D_MODEL = 2048
D_FF = 5632
NCH = D_MODEL // 128
NFF = D_FF // 128
NFH = NFF // 2
T = 512
SEQ = 16384
BATCH = 2
N_CORES = 8
TOK_CORE = BATCH * SEQ // N_CORES
HALO = 2048
RMS_EPS = 1e-6
NEGM = -1.0e6
DIL = (1, 4, 16)
SLOPES = [2.0 ** (-8.0 * (i + 1) / 12) for i in range(12)]
W_IN_COLS = 14848
CQ, CK, CV, CU, CGB, CGC, CGA, CGV = 0, 12, 24, 36, 52, 68, 84, 100

QUEUES = ("pe", "act", "dve", "pool", "sp")


class Op:
    __slots__ = ("q", "fn", "deps", "signal", "tick", "dma_key", "dma_val", "has_dependents")

    def __init__(self, q, fn):
        self.q = q
        self.fn = fn
        self.deps = []
        self.signal = False
        self.tick = 0
        self.dma_key = None
        self.dma_val = 0
        self.has_dependents = False


class Prog:
    def __init__(self, same_engine_sync=True):
        self.q = {k: [] for k in QUEUES}
        self.lastw = {}
        self.readers = {}
        self.dma_counts = {}
        self.same_engine_sync = same_engine_sync
        self.pending_bar = {}

    def barrier(self, key):
        for q in QUEUES:
            self.pending_bar[q] = key

    def add(self, q, fn, reads=(), writes=(), dma_key=None):
        op = Op(q, fn)
        if q in self.pending_bar:
            reads = list(reads) + [self.pending_bar.pop(q)]
        deps = {}
        for r in reads:
            w = self.lastw.get(r)
            if w is not None:
                deps[id(w)] = w
        for w_ in writes:
            w = self.lastw.get(w_)
            if w is not None:
                deps[id(w)] = w
            rd = self.readers.get(w_)
            if rd:
                for o in rd[0].values():
                    deps[id(o)] = o
                for o in rd[1]:
                    deps[id(o)] = o
        for d in deps.values():
            if d.dma_key is None and d.q == q:
                if q == "pe" or not self.same_engine_sync:
                    continue
            op.deps.append(d)
            d.has_dependents = True
        for r in reads:
            rd = self.readers.get(r)
            if rd is None:
                rd = ({}, [])
                self.readers[r] = rd
            if dma_key is not None:
                rd[1].append(op)
            else:
                rd[0][q] = op
        for w_ in writes:
            self.lastw[w_] = op
            self.readers[w_] = ({}, [])
        if dma_key is not None:
            op.dma_key = dma_key
            c = self.dma_counts.get(dma_key, 0) + 16
            self.dma_counts[dma_key] = c
            op.dma_val = c
        self.q[q].append(op)
        return op

    def assign_ticks(self):
        for qname in QUEUES:
            t = 0
            for op in self.q[qname]:
                if op.dma_key is None and op.has_dependents:
                    t += 1
                    op.signal = True
                    op.tick = t

    def emit_queue(self, qname, eng, eng_sems, dma_sems, final_dma_wait=False):
        known = {}
        for op in self.q[qname]:
            need = {}
            for d in op.deps:
                if d.dma_key is not None:
                    key = ("dma", d.dma_key)
                    val = d.dma_val
                    sem = dma_sems[d.dma_key]
                else:
                    key = ("q", d.q)
                    val = d.tick
                    sem = eng_sems[d.q]
                if known.get(key, 0) >= val:
                    continue
                if key not in need or need[key][1] < val:
                    need[key] = (sem, val)
            for key, (sem, val) in need.items():
                known[key] = val
                eng.wait_ge(sem, val)
            ins = op.fn(eng)
            if op.dma_key is not None:
                ins.then_inc(dma_sems[op.dma_key], 16)
            elif op.signal:
                ins.then_inc(eng_sems[qname], 1)
        if final_dma_wait:
            for key, cnt in self.dma_counts.items():
                if known.get(("dma", key), 0) < cnt:
                    eng.wait_ge(dma_sems[key], cnt)


def bc(ap, n):
    dims = list(ap.ap)
    return bass.AP(ap.tensor, ap.offset, [dims[0], (0, n)] + dims[1:])


def build_program(n_real=8, n_cores=8, same_engine_sync=True):
    n_halo = 4
    n_norm = n_real - n_halo
    assert n_norm >= 0
    nc = bass.Bass("TRN2", target_bir_lowering=False)
    P = Prog(same_engine_sync=same_engine_sync)

    def din(name, shape, dt=F32):
        return nc.dram_tensor(name, list(shape), dt, kind="ExternalInput").ap()

    x_d = din("x", [n_real * T, D_MODEL])
    wsrc = {
        "g1": din("ffn1_w_gate", [D_MODEL, D_FF]), "u1": din("ffn1_w_up", [D_MODEL, D_FF]),
        "d1": din("ffn1_w_down", [D_FF, D_MODEL]),
        "g2": din("ffn2_w_gate", [D_MODEL, D_FF]), "u2": din("ffn2_w_up", [D_MODEL, D_FF]),
        "d2": din("ffn2_w_down", [D_FF, D_MODEL]),
        "in": din("w_in", [D_MODEL, W_IN_COLS]),
        "ao": din("w_attn_out", [512, D_MODEL]),
        "co": din("w_conv_out", [D_MODEL, D_MODEL]),
        "o": din("w_o", [D_MODEL, D_MODEL]),
    }
    vecs_d = din("vecs", [128, 120])
    tabs_d = din("tabs", [128, 576])
    kb_d = din("kb", [128, 4 * n_real + 4])
    idx_d = din("idx", [1, 1], mybir.dt.int32)
    ident_d = din("ident", [128, 128])
    out_d = nc.dram_tensor("out", [n_real * T, D_MODEL], F32, kind="ExternalOutput").ap()
    x1s = nc.dram_tensor("x1s", [n_halo, 128, NCH * T], F32).ap()

    def dscr(name, nchunks, nk):
        return nc.dram_tensor(name, [nchunks, 128, nk * 128], BF16, kind="Internal").ap()

    wbf = {
        "g1": dscr("wbf_g1", NFF, NCH), "u1": dscr("wbf_u1", NFF, NCH),
        "d1": dscr("wbf_d1", NCH * 2, NFH),
        "g2": dscr("wbf_g2", NFF, NCH), "u2": dscr("wbf_u2", NFF, NCH),
        "d2": dscr("wbf_d2", NCH * 2, NFH),
        "in": dscr("wbf_in", 116, NCH), "ao": dscr("wbf_ao", NCH, 4),
        "co": dscr("wbf_co", NCH, NCH), "o": dscr("wbf_o", NCH, NCH),
    }

    es = ExitStack()
    with es:
        def sb(name, shape, dt):
            return es.enter_context(nc.sbuf_tensor("sb_" + name, list(shape), dt))

        xT = sb("xT", [128, NCH * T], F32)
        hT = sb("hT", [128, NCH * T], BF16)
        big = sb("big", [128, 32 * T], BF16)
        bigf = big.bitcast(F32)
        attnT = sb("attnT", [128, 4 * T], BF16)
        K1 = sb("K1", [128, 4 * 640], BF16)
        V1 = sb("V1", [128, 4 * 640], BF16)
        K2 = sb("K2", [128, 4 * 4 * 256], BF16)
        V2 = sb("V2", [128, 4 * 4 * 256], BF16)
        K3 = sb("K3", [128, 4 * 16 * 160], BF16)
        V3 = sb("V3", [128, 4 * 16 * 160], BF16)
        NWU = 7
        WU = 2048
        wbuf = sb("wbuf", [128, NWU * WU], BF16)
        NSCR = 8
        SCRW = 520
        scr = sb("scr", [128, NSCR * SCRW], F32)
        scrb = scr.bitcast(BF16)
        ident_f = sb("ident_f", [128, 128], F32)
        ident_b = sb("ident_b", [128, 128], BF16)
        ones_b = sb("ones_b", [128, 128], BF16)
        zeros_b = sb("zeros_b", [128, 128], BF16)
        vecs = sb("vecs", [128, 120], F32)
        tabs = sb("tabs", [128, 576], F32)
        kbt = sb("kbt", [128, 4 * n_real + 4], F32)
        idx_sb = sb("idx_sb", [1, 1], mybir.dt.int32)
        cc_sem = es.enter_context(nc.semaphore("cc_sem"))
        carry = sb("carry", [128, NCH * 2], F32)
        cst = sb("cst", [128, 4], F32)

        psb = [es.enter_context(nc.psum_tensor("ps%d" % i, [128, 512], F32)) for i in range(8)]
        psb_b = [p.bitcast(BF16) for p in psb]

        eng_sems = {q: es.enter_context(nc.semaphore("s_" + q)) for q in ("pe", "act", "dve", "pool")}
        dma_keys = []

        st = {"bank": 0, "scr": 0, "wu": 0, "abank": 0, "tile": 0}

        def bank(pool8=True):
            if pool8:
                b = st["bank"] % 8
                st["bank"] += 1
            else:
                b = st["abank"] % 4
                st["abank"] += 1
            return b

        def newscr():
            s = st["scr"] % NSCR
            st["scr"] += 1
            return s

        def scr_f(s, n=T, off=0):
            return scr[:, s * SCRW + off: s * SCRW + off + n]

        def scr_b(s, n=T, off=0):
            return scrb[:, s * 2 * SCRW + off: s * 2 * SCRW + off + n]

        def big_b(g0, n):
            return big[:, g0 * T: g0 * T + n]

        def gran(g0, nbytes):
            return [("big", g0 + i) for i in range((nbytes + 1023) // 1024)]

        def xc(c):
            return xT[:, c * T:(c + 1) * T]

        def hc(c):
            return hT[:, c * T:(c + 1) * T]

        def vcol(i):
            return vecs[:, i:i + 1]

        def wload(wname, chunk0, nchunks, nk):
            per = nk * 128
            units_per = (per + WU - 1) // WU
            nun = units_per * nchunks
            u0 = st["wu"] % NWU
            if u0 + nun > NWU:
                u0 = 0
            st["wu"] = u0 + nun
            keys = [("w", u0 + i) for i in range(nun)]
            dkey = ("w", u0)
            if dkey not in dma_keys:
                dma_keys.append(dkey)
            src = wbf[wname][chunk0:chunk0 + nchunks, :, :].rearrange("c p e -> p c e")
            if units_per * WU == per:
                dst = wbuf[:, u0 * WU:(u0 + nun) * WU].rearrange("p (c e) -> p c e", c=nchunks)
            else:
                dst = wbuf[:, u0 * WU:(u0 + nun) * WU].rearrange("p (c e) -> p c e", c=nchunks)[:, :, 0:per]
            chunks = list(range(chunk0, chunk0 + nchunks))
            if st["tile"] == 0 and all((wname, ch) not in cast_done for ch in chunks):
                skeys = []
                for i, ch in enumerate(chunks):
                    base = (u0 + i * units_per) * WU
                    for pi, (psrc, _pdst, k) in enumerate(cast_pieces(wname, ch, nk)):
                        off = base + (11 * 128 * pi if nk == NFH else 0)
                        lv = wbuf[:, off: off + k * 128].rearrange("p (k m) -> p k m", k=k)
                        uk = u0 + i * units_per + pi
                        dk = ("wd", uk)
                        if dk not in dma_keys:
                            dma_keys.append(dk)
                        P.add("pool", lambda e, lv=lv, psrc=psrc: e.dma_start(out=lv, in_=psrc), writes=[("w", uk)], dma_key=dk)
                    cast_done[(wname, ch)] = [("wbf", wname, ch)]
                    skeys.append(("wbf", wname, ch))
                sk = ("ws", u0)
                if sk not in dma_keys:
                    dma_keys.append(sk)
                P.add("sp", lambda e, dst=dst, src=src: e.dma_start(out=src, in_=dst), reads=keys, writes=skeys, dma_key=sk)
            else:
                ckeys = ensure_cast(wname, chunks, nk)
                P.add("sp", lambda e, dst=dst, src=src: e.dma_start(out=dst, in_=src), reads=ckeys, writes=keys, dma_key=dkey)
            aps = []
            for i in range(nchunks):
                base = (u0 + i * units_per) * WU
                aps.append(wbuf[:, base:base + per].rearrange("p (k m) -> p k m", k=nk))
            return aps, keys

        def mm_group(pb, wap, nk, rhs_fn, reads, col0=0, ncol=T):
            def fn(e, pb=pb, wap=wap, nk=nk, rhs_fn=rhs_fn):
                last = None
                for k in range(nk):
                    last = e.matmul(psb[pb][:, col0:col0 + ncol], wap[:, k, :], rhs_fn(k),
                                    start=(k == 0), stop=(k == nk - 1))
                return last
            P.add("pe", fn, reads=reads, writes=[("ps", pb)])

        def mm_multi(outs, nk, rhs_fn, rhs_key_fn, wkeys):
            for k in range(nk):
                def fn(e, k=k):
                    last = None
                    for pb, wap in outs:
                        last = e.matmul(psb[pb][:, :], wap[:, k, :], rhs_fn(k), start=(k == 0), stop=(k == nk - 1))
                    return last
                P.add("pe", fn, reads=[rhs_key_fn(k)] + wkeys, writes=[("ps", pb) for pb, _ in outs])

        def cload(dst, src, key):
            dma_keys.append(key)
            P.add("sp", lambda e: e.dma_start(out=dst, in_=src), writes=[key], dma_key=key)

        cload(ident_f[:], ident_d[:, :], "c_ident")
        cload(vecs[:], vecs_d[:, :], "c_vecs")
        cload(tabs[:], tabs_d[:, :], "c_tabs")
        cload(kbt[:], kb_d[:, :], "c_kb")
        cload(idx_sb[:], idx_d[:, :], "c_idx")
        P.add("pool", lambda e: e.tensor_copy(ident_b[:], ident_f[:]), reads=["c_ident"], writes=["ident_b"])
        P.add("pool", lambda e: e.memset(ones_b[:], 1.0), writes=["ones_b"])
        P.add("pool", lambda e: e.memset(zeros_b[:], 0.0), writes=["zeros_b"])
        P.add("pool", lambda e: e.memset(cst[:, 0:1], RMS_EPS), writes=["cst0"])
        P.add("pool", lambda e: e.memset(cst[:, 1:2], 128.0 * RMS_EPS), writes=["cst1"])
        P.add("pool", lambda e: e.memset(cst[:, 2:3], 0.0), writes=["cst2"])
        P.add("pool", lambda e: e.memset(carry[:], 0.0), writes=["carry"])
        for ci_, t_ in enumerate((K1, V1, K2, V2, K3, V3)):
            P.add("pool", lambda e, t_=t_: e.memset(t_[:], 0.0), writes=["cache%d" % ci_])
        CONST_READS = ["c_ident", "c_vecs", "c_tabs", "c_kb", "c_idx", "ident_b", "ones_b", "zeros_b", "cst0", "cst1", "cst2"]

        NLAND = 2
        land = sb("land", [128, NLAND * 2048], BF16)
        for k_ in range(NLAND):
            dma_keys.append(("ci", k_))
            dma_keys.append(("co", k_))
        cast_done = {}
        cstate = {"n": 0}
        BAR = CONST_READS + ["cache%d" % i for i in range(6)] + ["carry"]
        P.add("pool", lambda e: e.memset(cst[:, 3:4], 0.0), reads=BAR, writes=["bar"])
        P.barrier("bar")

        def cast_pieces(wname, chunk, nk):
            if nk == NFH:
                c, half = chunk // 2, chunk % 2
                out = []
                for q0 in (0, 11):
                    kk0 = half * NFH + q0
                    src = wsrc[wname].rearrange("(k p) n -> p k n", p=128)[:, kk0:kk0 + 11, c * 128:(c + 1) * 128]
                    dst = wbf[wname][chunk, :, q0 * 128:(q0 + 11) * 128]
                    out.append((src, dst, 11))
                return out
            src = wsrc[wname].rearrange("(k p) n -> p k n", p=128)[:, :, chunk * 128:(chunk + 1) * 128]
            return [(src, wbf[wname][chunk, :, :], nk)]

        def ensure_cast(wname, chunks, nk):
            keys = []
            todo = []
            for ch in chunks:
                if (wname, ch) not in cast_done:
                    pcs = cast_pieces(wname, ch, nk)
                    ks = [("wbf", wname, ch, i) for i in range(len(pcs))]
                    cast_done[(wname, ch)] = ks
                    for (src, dst, k), key in zip(pcs, ks):
                        todo.append((src, dst, k, key))
                keys += cast_done[(wname, ch)]
            for g0 in range(0, len(todo), NLAND):
                grp = todo[g0:g0 + NLAND]
                slots = []
                for (src, dst, k, key) in grp:
                    s_ = cstate["n"] % NLAND
                    cstate["n"] += 1
                    slots.append(s_)
                    lv = land[:, s_ * 2048: s_ * 2048 + k * 128].rearrange("p (k m) -> p k m", k=k)
                    P.add("pool", lambda e, lv=lv, src=src: e.dma_start(out=lv, in_=src), writes=[("land", s_)], dma_key=("ci", s_))
                for (src, dst, k, key), s_ in zip(grp, slots):
                    lv2 = land[:, s_ * 2048: s_ * 2048 + k * 128]
                    P.add("pool", lambda e, lv2=lv2, dst=dst: e.dma_start(out=dst, in_=lv2), reads=[("land", s_)], writes=[key], dma_key=("co", s_))
            return keys

        def RB(q):
            return []

        hTf = hT.bitcast(F32)

        def xstage(b):
            if b < 2:
                return hTf[:, b * 2048:(b + 1) * 2048], [("h", 8 * b + i) for i in range(8)]
            return bigf[:, (b - 2) * 2048:(b - 1) * 2048], gran(8 * (b - 2), 8192)

        def load_x_dma(j, blocks, q):
            for b in blocks:
                key = ("xl", b)
                if key not in dma_keys:
                    dma_keys.append(key)
                dst, keys = xstage(b)
                P.add(q, lambda e, b=b, dst=dst: e.dma_start(out=dst, in_=x_d[j * T + b * 128: j * T + (b + 1) * 128, :]),
                      writes=keys, dma_key=key)

        def load_x_transposes():
            for half in range(2):
                for c in range(NCH):
                    pb = bank()
                    rk = []
                    for b in (2 * half, 2 * half + 1):
                        rk += xstage(b)[1]
                    def fn(e, c=c, pb=pb, half=half):
                        last = None
                        for bb in range(2):
                            src = xstage(2 * half + bb)[0]
                            last = e.transpose(psb[pb][:, bb * 128:(bb + 1) * 128], src[:, c * 128:(c + 1) * 128], ident_f[:])
                        return last
                    P.add("pe", fn, reads=rk, writes=[("ps", pb)])
                    dst = xT[:, c * T + half * 256: c * T + (half + 1) * 256]
                    if c % 2 == 0:
                        P.add("act", lambda e, dst=dst, pb=pb: e.copy(dst, psb[pb][:, 0:256]), reads=[("ps", pb)], writes=[("x", c)])
                    else:
                        P.add("dve", lambda e, dst=dst, pb=pb: e.tensor_copy(dst, psb[pb][:, 0:256]), reads=[("ps", pb)], writes=[("x", c)])

        def store_x(jr, x1_next=None):
            stage = bigf
            order = [(b, g4) for b in range(4) for g4 in range(4)] if x1_next is None else \
                    [(b, g4) for g4 in range(4) for b in range(4)]
            for n_, (b, g4) in enumerate(order):
                pb = bank()
                def fn(e, b=b, g4=g4, pb=pb):
                    last = None
                    for cc in range(4):
                        c = g4 * 4 + cc
                        last = e.transpose(psb[pb][:, cc * 128:(cc + 1) * 128],
                                           xT[:, c * T + b * 128: c * T + (b + 1) * 128], ident_f[:])
                    return last
                P.add("pe", fn, reads=[("x", g4 * 4 + cc) for cc in range(4)], writes=[("ps", pb)])
                dst = stage[:, b * 2048 + g4 * 512: b * 2048 + (g4 + 1) * 512]
                grs = gran(8 * b + 2 * g4, 2048)
                if n_ % 2 == 0:
                    P.add("act", lambda e, dst=dst, pb=pb: e.copy(dst, psb[pb][:, :]), reads=[("ps", pb)], writes=grs)
                else:
                    P.add("dve", lambda e, dst=dst, pb=pb: e.tensor_copy(dst, psb[pb][:, :]), reads=[("ps", pb)], writes=grs)
                if x1_next is None and g4 == 3:
                    store_dma(jr, b)
                if x1_next is not None and b == 3:
                    x1_load(x1_next, (g4,))
            if x1_next is not None:
                for b in range(4):
                    store_dma(jr, b)

        def store_dma(jr, b):
            key = ("xs", b)
            if key not in dma_keys:
                dma_keys.append(key)
            P.add("act", lambda e: e.dma_start(out=out_d[jr * T + b * 128: jr * T + (b + 1) * 128, :],
                                                in_=bigf[:, b * 2048:(b + 1) * 2048]),
                  reads=gran(8 * b, 8192), dma_key=key)

        def rms_norm(gain_base):
            pb = bank()
            sq = []
            for c in range(NCH):
                s = newscr()
                sq.append(s)
                P.add("act", lambda e, c=c, s=s: e.activation(scr_b(s), xc(c), AF.Square),
                      reads=[("x", c)], writes=[("scr", s)])
                P.add("pe", lambda e, c=c, s=s, pb=pb: e.matmul(psb[pb][:, :], ones_b[:], scr_b(s),
                                                                 start=(c == 0), stop=(c == NCH - 1)),
                      reads=[("scr", s)], writes=[("ps", pb)])
            s1 = newscr()
            P.add("act", lambda e: e.activation(scr_f(s1), psb[pb][:, :], AF.Ln, bias=cst[:, 0:1], scale=1.0 / D_MODEL),
                  reads=[("ps", pb)], writes=[("scr", s1)])
            P.add("act", lambda e: e.activation(scr_f(s1), scr_f(s1), AF.Exp, scale=-0.5), reads=[("scr", s1)], writes=[("scr", s1)])
            for c in range(NCH):
                P.add("dve", lambda e, c=c: e.scalar_tensor_tensor(hc(c), xc(c), vcol(gain_base + c), scr_f(s1), ALU.mult, ALU.mult),
                      reads=[("x", c), ("scr", s1)], writes=[("h", c)])

        def ffn(wg, wu, wd, after_gateup=None, on_chunk=None):
            HREADS = [("h", c) for c in range(NCH)]
            for half in range(2):
                for i in range(NFH // 2):
                    bi = half * NFH + 2 * i
                    gaps, gk = wload(wg, bi, 2, NCH)
                    uaps, uk = wload(wu, bi, 2, NCH)
                    first = (half == 0 and i == 0)
                    if first:
                        fb = [bank() for _ in range(4)]
                        mm_multi([(fb[0], gaps[0]), (fb[1], uaps[0]), (fb[2], gaps[1]), (fb[3], uaps[1])], NCH,
                                 lambda k: hc(k), lambda k: ("h", k), gk + uk)
                    for sub in range(2):
                        m = 2 * i + sub
                        if first:
                            pg, pu = fb[2 * sub], fb[2 * sub + 1]
                        else:
                            pg = bank()
                            mm_group(pg, gaps[sub], NCH, lambda k: hc(k), HREADS + gk)
                            pu = bank()
                            mm_group(pu, uaps[sub], NCH, lambda k: hc(k), HREADS + uk)
                        s = newscr()
                        P.add("act", lambda e, s=s, pg=pg: e.activation(scr_f(s), psb[pg][:, :], AF.Silu),
                              reads=[("ps", pg)], writes=[("scr", s)])
                        P.add("dve", lambda e, s=s, pu=pu, m=m: e.tensor_tensor(big_b(m, T), scr_f(s), psb[pu][:, :], ALU.mult),
                              reads=[("scr", s), ("ps", pu)], writes=[("big", m)])
                if half == 1 and after_gateup is not None:
                    after_gateup()
                for c in range(NCH):
                    daps, dk = wload(wd, c * 2 + half, 1, NFH)
                    pd = bank()
                    mm_group(pd, daps[0], NFH, lambda k: big_b(k, T), [("big", m) for m in range(NFH)] + dk)
                    P.add("dve", lambda e, c=c, pd=pd: e.scalar_tensor_tensor(xc(c), psb[pd][:, :], 0.5, xc(c), ALU.mult, ALU.add),
                          reads=[("ps", pd), ("x", c)], writes=[("x", c)])
                    if half == 1 and on_chunk is not None:
                        on_chunk(c)

        def kv_dst(cache, g, hh, j):
            if g == 0:
                return cache[:, hh * 640 + 128: hh * 640 + 640]
            if g == 1:
                s2 = j % 2
                v = cache[:, hh * 1024:(hh + 1) * 1024].rearrange("p (r s) -> p r s", r=4)
                return v[:, :, s2 * 128:(s2 + 1) * 128]
            s5 = j % 5
            v = cache[:, hh * 2560:(hh + 1) * 2560].rearrange("p (r s) -> p r s", r=16)
            return v[:, :, s5 * 32:(s5 + 1) * 32]

        def perm(ap, g):
            if g == 0:
                return ap
            return ap.rearrange("p (i r) -> p r i", r=DIL[g])

        CACHES = ((K1, V1), (K2, V2), (K3, V3))

        def head_post(pk, gain_col, is_q, dst, dst_keys, g, bankfn):
            s = newscr()
            P.add("act", lambda e: e.activation(scr_b(s), psb[pk][:, :], AF.Square), reads=[("ps", pk)], writes=[("scr", s)])
            pss = bankfn()
            P.add("pe", lambda e: e.matmul(psb[pss][:, :], ones_b[:], scr_b(s), start=True, stop=True),
                  reads=[("scr", s)], writes=[("ps", pss)])
            s2 = newscr()
            if is_q:
                P.add("act", lambda e: e.activation(scr_f(s2), psb[pss][:, :], AF.Ln, bias=cst[:, 1:2], scale=1.0),
                      reads=[("ps", pss)], writes=[("scr", s2)])
            else:
                P.add("act", lambda e: e.activation(scr_f(s2), psb[pss][:, :], AF.Ln, bias=cst[:, 0:1], scale=1.0 / 128.0),
                      reads=[("ps", pss)], writes=[("scr", s2)])
            P.add("act", lambda e: e.activation(scr_f(s2), scr_f(s2), AF.Exp, scale=-0.5), reads=[("scr", s2)], writes=[("scr", s2)])
            P.add("dve", lambda e: e.scalar_tensor_tensor(dst, perm(psb[pk][:, :], g), gain_col, perm(scr_f(s2), g), ALU.mult, ALU.mult),
                  reads=[("ps", pk), ("scr", s2)], writes=dst_keys)

        def kvq(j, hh, with_q, qset, bankfn, first, groups=(0, 1, 2)):
            HREADS = [("h", c) for c in range(NCH)]
            items = []
            for g in groups:
                aps, wk = wload("in", CK + 4 * g + hh, 1, NCH)
                items.append(("k", g, aps[0], wk))
            if with_q:
                for g in range(3):
                    aps, wk = wload("in", CQ + 4 * g + hh, 1, NCH)
                    items.append(("q", g, aps[0], wk))
            banks = []
            nfirst = 4 if (first and with_q) else (len(groups) if first else 0)
            if nfirst:
                fb = [bankfn() for _ in range(nfirst)]
                allk = []
                for it in items[:nfirst]:
                    allk += it[3]
                mm_multi([(fb[i], items[i][2]) for i in range(nfirst)], NCH, lambda k: hc(k), lambda k: ("h", k), allk)
            for idx, (kind, g, wap, wk) in enumerate(items):
                if idx < nfirst:
                    pk = fb[idx]
                else:
                    pk = bankfn()
                    mm_group(pk, wap, NCH, lambda k: hc(k), HREADS + wk)
                hd = 4 * g + hh
                if kind == "k":
                    head_post(pk, vcol(60 + hd), False, kv_dst(CACHES[g][0], g, hh, j), [("K", g, hh)], g, bankfn)
                else:
                    dst = big_b(qset * 3 + g, T)
                    if g > 0:
                        dst = dst.rearrange("p (r i) -> p r i", r=DIL[g])
                    head_post(pk, vcol(48 + hd), True, dst, [("big", qset * 3 + g)], g, bankfn)
            for g in groups:
                aps, wk = wload("in", CV + 4 * g + hh, 1, NCH)
                pv = bankfn()
                mm_group(pv, aps[0], NCH, lambda k: hc(k), HREADS + wk)
                P.add("act", lambda e, pv=pv, g=g: e.copy(kv_dst(CACHES[g][1], g, hh, j), perm(psb[pv][:, :], g)),
                      reads=[("ps", pv)], writes=[("V", g, hh)])

        def attention(j, jr):
            s2 = j % 2
            ph = j % 5
            VS1 = 12 * T
            VS2 = VS1 + 5 * 128
            VS3X = VS2 + 8 * 128
            VS3Y = VS3X + 16 * 128
            VSKEYS = [("big", i) for i in range(12, 24)]
            PB = 6 * T
            PG = 6
            def pbuf(i):
                return big[:, PB + i * T: PB + (i + 1) * T]
            def attn_head(hh):
                qs = hh % 2
                k1 = K1[:, hh * 640:(hh + 1) * 640]
                v1 = V1[:, hh * 640:(hh + 1) * 640]
                k2 = K2[:, hh * 1024:(hh + 1) * 1024].rearrange("p (r s) -> p r s", r=4)
                v2 = V2[:, hh * 1024:(hh + 1) * 1024].rearrange("p (r s) -> p r s", r=4)
                k3 = K3[:, hh * 2560:(hh + 1) * 2560].rearrange("p (r s) -> p r s", r=16)
                v3 = V3[:, hh * 2560:(hh + 1) * 2560].rearrange("p (r s) -> p r s", r=16)
                q1 = big_b(qs * 3 + 0, T)
                q2 = big_b(qs * 3 + 1, T).rearrange("p (r i) -> p r i", r=4)
                q3 = big_b(qs * 3 + 2, T).rearrange("p (r i) -> p r i", r=16)
                KR = [[("K", g, hh)] for g in range(3)]
                VR = [[("V", g, hh)] for g in range(3)]
                sb_ = [bank(False) for _ in range(6)]
                def s_g1(e, kind, pb):
                    last = None
                    for c in range(4):
                        kb0 = 128 * c if kind == 0 else 128 * (c + 1)
                        last = e.matmul(psb[pb][:, c * 128:(c + 1) * 128], k1[:, kb0:kb0 + 128], q1[:, c * 128:(c + 1) * 128],
                                        start=True, stop=True)
                    return last
                def s_g2(e, kind, pb):
                    last = None
                    so = (1 - s2) if kind == 0 else s2
                    for r in range(4):
                        last = e.matmul(psb[pb][:, r * 128:(r + 1) * 128], k2[:, r, so * 128:(so + 1) * 128], q2[:, r, :],
                                        start=True, stop=True)
                    return last
                def s_g3(e, kind, pb):
                    last = None
                    for r in range(16):
                        if kind == 0:
                            last = e.matmul(psb[pb][:, r * 32:(r + 1) * 32], k3[:, r, 0:128], q3[:, r, :], start=True, stop=True)
                        else:
                            last = e.matmul(psb[pb][0:32, r * 32:(r + 1) * 32], k3[:, r, 128:160], q3[:, r, :], start=True, stop=True)
                    return last
                sfn = [lambda e, pb=sb_[0]: s_g1(e, 0, pb), lambda e, pb=sb_[1]: s_g1(e, 1, pb),
                       lambda e, pb=sb_[2]: s_g2(e, 0, pb), lambda e, pb=sb_[3]: s_g2(e, 1, pb),
                       lambda e, pb=sb_[4]: s_g3(e, 0, pb), lambda e, pb=sb_[5]: s_g3(e, 1, pb)]
                sg = [0, 0, 1, 1, 2, 2]
                qg = [qs * 3, qs * 3, qs * 3 + 1, qs * 3 + 1, qs * 3 + 2, qs * 3 + 2]
                Dt = tabs
                tviews = [bc(Dt[:, 128:256], 4), bc(Dt[:, 0:128], 4), bc(Dt[:, 128:256], 4), bc(Dt[:, 0:128], 4),
                          bc(Dt[:, 256 + 64 * ph: 256 + 64 * ph + 32], 16), bc(Dt[0:32, 256 + 64 * ph + 32: 256 + 64 * ph + 64], 16)]
                shp = [(128, 4), (128, 4), (128, 4), (128, 4), (128, 16), (32, 16)]
                for i in range(6):
                    g = sg[i]
                    cval = SLOPES[4 * g + hh] * DIL[g]
                    P.add("pe", sfn[i], reads=KR[g] + [("big", qg[i])], writes=[("ps", sb_[i])])
                    s = newscr()
                    np_, nu = shp[i]
                    tmp = scr_f(s)[0:np_, :].rearrange("p (u n) -> p u n", u=nu)
                    sin = psb[sb_[i]][0:np_, :].rearrange("p (u n) -> p u n", u=nu)
                    P.add("dve", lambda e, tmp=tmp, sin=sin, tv=tviews[i], cval=cval: e.scalar_tensor_tensor(tmp, tv, cval, sin, ALU.mult, ALU.add),
                          reads=[("ps", sb_[i])], writes=[("scr", s)])
                    pdst = pbuf(i)
                    pkey = [("big", PG + i)]
                    if i == 0:
                        P.add("act", lambda e, s=s, pdst=pdst: e.activation(pdst[:, 0:128], scr_f(s)[:, 0:128], AF.Exp, bias=kbt[:, 4 * jr: 4 * jr + 1]),
                              reads=[("scr", s)], writes=pkey)
                        P.add("act", lambda e, s=s, pdst=pdst: e.activation(pdst[:, 128:512], scr_f(s)[:, 128:512], AF.Exp),
                              reads=[("scr", s)], writes=[("big", PG + i, "b")])
                    elif i == 2:
                        P.add("act", lambda e, s=s, pdst=pdst: e.activation(pdst[:, :], scr_f(s), AF.Exp, bias=kbt[:, 4 * jr + 1: 4 * jr + 2]),
                              reads=[("scr", s)], writes=pkey)
                    elif i == 4:
                        P.add("act", lambda e, s=s, pdst=pdst: e.activation(pdst[:, :], scr_f(s), AF.Exp, bias=kbt[:, 4 * jr + 2: 4 * jr + 3]),
                              reads=[("scr", s)], writes=pkey)
                    elif i == 5:
                        P.add("act", lambda e, s=s, pdst=pdst: e.activation(pdst[0:32, :], scr_f(s)[0:32, :], AF.Exp, bias=kbt[0:32, 4 * jr + 3: 4 * jr + 4]),
                              reads=[("scr", s)], writes=pkey)
                    else:
                        P.add("act", lambda e, s=s, pdst=pdst: e.activation(pdst[:, :], scr_f(s), AF.Exp),
                              reads=[("scr", s)], writes=pkey)
                def tr_batch(srcs, np_, dst_off, evq):
                    for b0 in range(0, len(srcs), 8):
                        grp = srcs[b0:b0 + 8]
                        pb = bank(False)
                        def fn(e, grp=grp, pb=pb):
                            last = None
                            for t_, sap in enumerate(grp):
                                last = e.transpose(psb_b[pb][0:np_, t_ * 128:(t_ + 1) * 128], sap, ident_b[:])
                            return last
                        P.add("pe", fn, reads=VR[0] + VR[1] + VR[2], writes=[("ps", pb)])
                        n = len(grp) * 128
                        dst = big[0:np_, dst_off + b0 * 128: dst_off + b0 * 128 + n]
                        if evq == "act":
                            P.add("act", lambda e, dst=dst, pb=pb, n=n: e.copy(dst, psb_b[pb][0:np_, 0:n]), reads=[("ps", pb)], writes=VSKEYS)
                        else:
                            P.add("dve", lambda e, dst=dst, pb=pb, n=n: e.tensor_copy(dst, psb_b[pb][0:np_, 0:n]), reads=[("ps", pb)], writes=VSKEYS)
                tr_batch([v1[:, 128 * b:128 * (b + 1)] for b in range(5)], 128, VS1, "act")
                tr_batch([v2[:, r, so * 128:(so + 1) * 128] for r in range(4) for so in range(2)], 128, VS2, "act")
                tr_batch([v3[:, r, 0:128] for r in range(16)], 128, VS3X, "act")
                tr_batch([v3[:, r, 128:160] for r in range(16)], 32, VS3Y, "act")
                return lambda: attn_head_b(hh)

            def attn_head_b(hh):
                VS1_, VS2_, VS3X_, VS3Y_ = VS1, VS2, VS3X, VS3Y
                pN = 4 + bank(False)
                pL = 4 + bank(False)
                P.add("pe", lambda e, pN=pN: e.matmul(psb[pN][:, :], zeros_b[:], hc(0), start=True, stop=False),
                      reads=[("h", 0)], writes=[("ps", pN)])
                P.add("pe", lambda e, pL=pL: e.matmul(psb[pL][:, :], zeros_b[:], hc(0), start=True, stop=False),
                      reads=[("h", 0)], writes=[("ps", pL)])
                pN_g2 = psb[pN][:, :].rearrange("p (i r) -> p r i", r=4)
                pL_g2 = psb[pL][:, :].rearrange("p (i r) -> p r i", r=4)
                pN_g3 = psb[pN][:, :].rearrange("p (i r) -> p r i", r=16)
                pL_g3 = psb[pL][:, :].rearrange("p (i r) -> p r i", r=16)
                def pv_fn(e, pN=pN, pL=pL, pN_g2=pN_g2, pL_g2=pL_g2, pN_g3=pN_g3, pL_g3=pL_g3):
                    vs1 = big[:, VS1:VS1 + 640].rearrange("p (b d) -> p b d", b=5)
                    vs2 = big[:, VS2:VS2 + 1024].rearrange("p (b d) -> p b d", b=8)
                    vs3x = big[:, VS3X:VS3X + 2048].rearrange("p (b d) -> p b d", b=16)
                    vs3y = big[0:32, VS3Y:VS3Y + 2048].rearrange("p (b d) -> p b d", b=16)
                    e.matmul(psb[pL][:, :], ones_b[:], pbuf(0)[:, :], start=False, stop=False)
                    e.matmul(psb[pL][:, :], ones_b[:], pbuf(1)[:, :], start=False, stop=False)
                    for kind in range(2):
                        e.matmul(pL_g2, ones_b[:], pbuf(2 + kind)[:, :].rearrange("p (r i) -> p r i", r=4), start=False, stop=False)
                    e.matmul(pL_g3, ones_b[:], pbuf(4)[:, :].rearrange("p (r i) -> p r i", r=16), start=False, stop=False)
                    e.matmul(pL_g3, ones_b[0:32, :], pbuf(5)[0:32, :].rearrange("p (r i) -> p r i", r=16), start=False, stop=True)
                    for c in range(4):
                        for kind in range(2):
                            pp = pbuf(kind)[:, c * 128:(c + 1) * 128]
                            vb = vs1[:, c + kind, :]
                            e.matmul(psb[pN][:, c * 128:(c + 1) * 128], vb, pp, start=False, stop=False)
                    for r in range(4):
                        for kind in range(2):
                            so = (1 - s2) if kind == 0 else s2
                            pp = pbuf(2 + kind)[:, r * 128:(r + 1) * 128]
                            vb = vs2[:, r * 2 + so, :]
                            e.matmul(pN_g2[:, r, :], vb, pp, start=False, stop=False)
                    last = None
                    for r in range(16):
                        pp = pbuf(4)[:, r * 32:(r + 1) * 32]
                        e.matmul(pN_g3[:, r, :], vs3x[:, r, :], pp, start=False, stop=False)
                        pp = pbuf(5)[0:32, r * 32:(r + 1) * 32]
                        last = e.matmul(pN_g3[:, r, :], vs3y[:, r, :], pp, start=False, stop=(r == 15))
                    return last
                P.add("pe", pv_fn, reads=VSKEYS + [("big", PG + i) for i in range(6)] + [("big", PG, "b")],
                      writes=[("ps", pN), ("ps", pL)])
                s = newscr()
                P.add("dve", lambda e, s=s, pL=pL: e.reciprocal(scr_f(s), psb[pL][:, :]), reads=[("ps", pL)], writes=[("scr", s)])
                P.add("dve", lambda e, s=s, pN=pN, hh=hh: e.tensor_tensor(attnT[:, hh * T:(hh + 1) * T], psb[pN][:, :], scr_f(s), ALU.mult),
                      reads=[("ps", pN), ("scr", s)], writes=[("attnT", hh)])
            kvq(j, 0, True, 0, bank, True)
            pend = None
            for hh in range(4):
                partb = attn_head(hh)
                if hh < 3:
                    kvq(j, hh + 1, True, (hh + 1) % 2, lambda: bank(False), False)
                partb()
            for hh in range(4):
                P.add("dve", lambda e, hh=hh: e.tensor_copy(K1[:, hh * 640: hh * 640 + 128], K1[:, hh * 640 + 512: hh * 640 + 640]),
                      reads=[("K", 0, hh)], writes=[("K", 0, hh)])
                P.add("dve", lambda e, hh=hh: e.tensor_copy(V1[:, hh * 640: hh * 640 + 128], V1[:, hh * 640 + 512: hh * 640 + 640]),
                      reads=[("V", 0, hh)], writes=[("V", 0, hh)])

        def roll_g1_only():
            for hh in range(4):
                P.add("dve", lambda e, hh=hh: e.tensor_copy(K1[:, hh * 640: hh * 640 + 128], K1[:, hh * 640 + 512: hh * 640 + 640]),
                      reads=[("K", 0, hh)] + RB("pool"), writes=[("K", 0, hh)])
                P.add("dve", lambda e, hh=hh: e.tensor_copy(V1[:, hh * 640: hh * 640 + 128], V1[:, hh * 640 + 512: hh * 640 + 640]),
                      reads=[("V", 0, hh)], writes=[("V", 0, hh)])

        def conv_branch(state_only):
            HREADS = [("h", c) for c in range(NCH)]
            for c in range(NCH):
                uaps, uk = wload("in", CU + c, 1, NCH)
                caps, ck = wload("in", CGC + c, 1, NCH)
                pu = bank()
                mm_group(pu, uaps[0], NCH, lambda k: hc(k), HREADS + uk)
                pc = bank()
                mm_group(pc, caps[0], NCH, lambda k: hc(k), HREADS + ck)
                if not state_only:
                    baps, bk = wload("in", CGB + c, 1, NCH)
                    pbk = bank()
                    mm_group(pbk, baps[0], NCH, lambda k: hc(k), HREADS + bk)
                su = newscr()
                P.add("act", lambda e, su=su, pu=pu: e.copy(scr_f(su), psb[pu][:, :]), reads=[("ps", pu)], writes=[("scr", su)])
                sz = newscr()
                P.add("dve", lambda e, sz=sz, c=c: e.tensor_copy(scr_f(sz, 2, 0), carry[:, 2 * c: 2 * c + 2]),
                      reads=["carry"], writes=[("scr", sz)])
                P.add("dve", lambda e, sz=sz, su=su, pc=pc: e.tensor_tensor(scr_f(sz, T, 2), psb[pc][:, :], scr_f(su), ALU.mult),
                      reads=[("ps", pc), ("scr", su), ("scr", sz)], writes=[("scr", sz)])
                P.add("dve", lambda e, sz=sz, c=c: e.tensor_copy(carry[:, 2 * c: 2 * c + 2], scr_f(sz, 2, T)),
                      reads=[("scr", sz)], writes=["carry"])
                if state_only:
                    continue
                sa = newscr()
                w0, w1, w2 = vcol(72 + 3 * c), vcol(72 + 3 * c + 1), vcol(72 + 3 * c + 2)
                P.add("dve", lambda e, sa=sa, sz=sz, w0=w0: e.tensor_scalar(scr_f(sa), scr_f(sz, T, 0), w0, None, ALU.mult),
                      reads=[("scr", sz)], writes=[("scr", sa)])
                P.add("dve", lambda e, sa=sa, sz=sz, w1=w1: e.scalar_tensor_tensor(scr_f(sa), scr_f(sz, T, 1), w1, scr_f(sa), ALU.mult, ALU.add),
                      reads=[("scr", sz), ("scr", sa)], writes=[("scr", sa)])
                P.add("dve", lambda e, sa=sa, sz=sz, w2=w2: e.scalar_tensor_tensor(scr_f(sa), scr_f(sz, T, 2), w2, scr_f(sa), ALU.mult, ALU.add),
                      reads=[("scr", sz), ("scr", sa)], writes=[("scr", sa)])
                P.add("dve", lambda e, sa=sa, pbk=pbk, c=c: e.tensor_tensor(big_b(c, T), psb[pbk][:, :], scr_f(sa), ALU.mult),
                      reads=[("ps", pbk), ("scr", sa)], writes=[("big", c)])

        def merge_and_out():
            HREADS = [("h", c) for c in range(NCH)]
            for c in range(NCH):
                aaps, ak = wload("ao", c, 1, 4)
                coaps, cok = wload("co", c, 1, NCH)
                gaaps, gak = wload("in", CGA + c, 1, NCH)
                gvaps, gvk = wload("in", CGV + c, 1, NCH)
                pA = bank()
                mm_group(pA, aaps[0], 4, lambda k: attnT[:, k * T:(k + 1) * T], [("attnT", k) for k in range(4)] + ak)
                pB = bank()
                mm_group(pB, coaps[0], NCH, lambda k: big_b(k, T), [("big", k) for k in range(NCH)] + cok)
                pGa = bank()
                mm_group(pGa, gaaps[0], NCH, lambda k: hc(k), HREADS + gak)
                pGv = bank()
                mm_group(pGv, gvaps[0], NCH, lambda k: hc(k), HREADS + gvk)
                s1, s2_ = newscr(), newscr()
                P.add("act", lambda e, s1=s1, pGa=pGa: e.activation(scr_f(s1), psb[pGa][:, :], AF.Sigmoid), reads=[("ps", pGa)], writes=[("scr", s1)])
                P.add("act", lambda e, s2_=s2_, pGv=pGv: e.activation(scr_f(s2_), psb[pGv][:, :], AF.Sigmoid), reads=[("ps", pGv)], writes=[("scr", s2_)])
                P.add("dve", lambda e, s1=s1, pA=pA: e.tensor_tensor(scr_f(s1), psb[pA][:, :], scr_f(s1), ALU.mult),
                      reads=[("ps", pA), ("scr", s1)], writes=[("scr", s1)])
                P.add("dve", lambda e, s2_=s2_, pB=pB: e.tensor_tensor(scr_f(s2_), psb[pB][:, :], scr_f(s2_), ALU.mult),
                      reads=[("ps", pB), ("scr", s2_)], writes=[("scr", s2_)])
                P.add("dve", lambda e, s1=s1, s2_=s2_, c=c: e.tensor_tensor(big_b(16 + c, T), scr_f(s1), scr_f(s2_), ALU.add),
                      reads=[("scr", s1), ("scr", s2_)], writes=[("big", 16 + c)])
            for c in range(NCH):
                oaps, ok = wload("o", c, 1, NCH)
                po = bank()
                mm_group(po, oaps[0], NCH, lambda k: big_b(16 + k, T), [("big", 16 + k) for k in range(NCH)] + ok)
                P.add("dve", lambda e, c=c, po=po: e.tensor_tensor(xc(c), psb[po][:, :], xc(c), ALU.add),
                      reads=[("ps", po), ("x", c)], writes=[("x", c)])

        def x1_store(t, c):
            if c % 4 != 3:
                return
            g4 = c // 4
            key = ("x1st", g4)
            if key not in dma_keys:
                dma_keys.append(key)
            P.add("act", lambda e: e.dma_start(out=x1s[t, :, g4 * 4 * T:(g4 + 1) * 4 * T], in_=xT[:, g4 * 4 * T:(g4 + 1) * 4 * T]),
                  reads=[("x", g4 * 4 + i) for i in range(4)], writes=[("x1s", t, g4)], dma_key=key)

        def x1_load(t, groups=(0, 1, 2, 3)):
            for g4 in groups:
                key = ("x1ld", g4)
                if key not in dma_keys:
                    dma_keys.append(key)
                P.add("act", lambda e, g4=g4: e.dma_start(out=xT[:, g4 * 4 * T:(g4 + 1) * 4 * T], in_=x1s[t, :, g4 * 4 * T:(g4 + 1) * 4 * T]),
                      reads=[("x1s", t, g4)], writes=[("x", g4 * 4 + i) for i in range(4)], dma_key=key)

        def kv_keys(kind, g):
            return [(kind, g, hh) for hh in range(4)]
        carry_b = carry.bitcast(BF16)
        k2s = K2[:, :].rearrange("p (a s) -> p a s", s=256)[:, :, 128:256]
        v2s = V2[:, :].rearrange("p (a s) -> p a s", s=256)[:, :, 128:256]
        k1s = K1[:, :].rearrange("p (a s) -> p a s", s=640)[:, :, 0:128]
        v1s = V1[:, :].rearrange("p (a s) -> p a s", s=640)[:, :, 0:128]
        BUFS = [
            [(K3[:, 0:8192], 0, 8192, 0, kv_keys("K", 2))],
            [(K3[:, 8192:10240], 0, 2048, 0, kv_keys("K", 2)), (V3[:, 0:6144], 2048, 6144, 0, kv_keys("V", 2))],
            [(V3[:, 6144:10240], 0, 4096, 0, kv_keys("V", 2)), (k2s, 4096, 2048, 16, kv_keys("K", 1)), (v2s, 6144, 2048, 16, kv_keys("V", 1))],
            [(k1s, 0, 512, 4, kv_keys("K", 0)), (v1s, 512, 512, 4, kv_keys("V", 0)), (carry_b[:, :], 1024, 64, 0, ["carry"])],
        ]
        BW = [sum(p[2] for p in bf) for bf in BUFS]
        sendbs = [nc.dram_tensor("sendb%d" % i, [128, BW[i]], BF16) for i in range(len(BUFS))]
        recvbs = [nc.dram_tensor("recvb%d" % i, [n_cores * 128, BW[i]], BF16) for i in range(len(BUFS))]

        def dview(ap2d, a):
            return ap2d if a == 0 else ap2d.rearrange("p (a b) -> p a b", a=a)

        def exchange_send():
            keys = []
            n_ = 0
            for i, bf in enumerate(BUFS):
                for (sap, o, n, a, ks) in bf:
                    key = ("snd", n_)
                    n_ += 1
                    dma_keys.append(key)
                    P.add("sp", lambda e, sap=sap, o=o, n=n, a=a, i=i: e.dma_start(out=dview(sendbs[i].ap()[:, o:o + n], a), in_=sap),
                          reads=ks, writes=[("sendb", key)], dma_key=key)
                    keys.append(("sendb", key))
            def coll_fn(g_):
                for i in range(len(BUFS)):
                    ins = g_.collective_compute("AllGather", ALU.bypass, replica_groups=[list(range(n_cores))],
                                                ins=[sendbs[i].ap().opt()], outs=[recvbs[i].ap().opt()])
                    ins.then_inc(cc_sem)
                    g_.wait_ge(cc_sem, i + 1)
                return g_.memset(cst[:, 3:4], 0.0)
            P.add("pool", coll_fn, reads=keys, writes=["recvb"])

        xs_ = {}

        def exchange_recv():
            n_ = 0
            for i, bf in enumerate(BUFS):
                for (sap, o, n, a, ks) in bf:
                    key = ("rcv", n_)
                    dma_keys.append(key)
                    def fn(e, sap=sap, o=o, n=n, a=a, i=i, first=(n_ == 0)):
                        if first:
                            reg = e.alloc_register("ridx")
                            e.reg_load(reg, idx_sb[0:1, 0:1])
                            xs_["val"] = e.snap(reg, min_val=0, max_val=n_cores - 1)
                        return e.dma_start(out=sap, in_=dview(recvbs[i].ap()[bass.ds(xs_["val"] * 128, 128), o:o + n], a))
                    n_ += 1
                    P.add("sp", fn, reads=["recvb"], writes=ks, dma_key=key)
            P.add("dve", lambda e: e.tensor_scalar(carry[:], carry[:], kbt[:, 4 * n_real: 4 * n_real + 1], None, ALU.mult),
                  reads=["carry"], writes=["carry"])

        seq = 0
        load_x_dma(n_norm, (0, 1, 2, 3), "act")
        for t in range(n_halo):
            st["tile"] = seq
            seq += 1
            last_halo = (t == n_halo - 1)
            load_x_transposes()
            rms_norm(0)
            ffn("g1", "u1", "d1", on_chunk=lambda c, t=t: x1_store(t, c))
            rms_norm(16)
            grp = (0, 1, 2) if last_halo else (2,)
            for hh in range(4):
                kvq(t, hh, False, 0, bank, hh == 0, groups=grp)
            if last_halo:
                conv_branch(True)
                roll_g1_only()
            nxt = n_norm + t + 1 if t + 1 < n_halo else (0 if n_norm > 0 else None)
            if nxt is not None:
                load_x_dma(nxt, (2, 3), "sp")
                load_x_dma(nxt, (0, 1), "act")
            if t == 0:
                for g in range(2):
                    for c in range(4):
                        ensure_cast("in", [CK + 4 * g + c], NCH)
                        ensure_cast("in", [CV + 4 * g + c], NCH)
                for c in range(4):
                    for g in range(3):
                        ensure_cast("in", [CQ + 4 * g + c], NCH)
                for c in range(NCH):
                    ensure_cast("in", [CU + c], NCH)
                    ensure_cast("in", [CGC + c], NCH)
                    ensure_cast("in", [CGB + c], NCH)
                for c in range(NCH):
                    ensure_cast("ao", [c], 4)
                    ensure_cast("co", [c], NCH)
                    ensure_cast("in", [CGA + c], NCH)
                    ensure_cast("in", [CGV + c], NCH)
                for c in range(NCH):
                    ensure_cast("o", [c], NCH)
                for c in range(0, 16, 2):
                    ensure_cast("g2", [c, c + 1], NCH)
                    ensure_cast("u2", [c, c + 1], NCH)
        exchange_send()

        for rt in range(n_real):
            j = n_halo + rt
            st["tile"] = seq
            seq += 1
            own_ffn1 = rt < n_norm
            if own_ffn1:
                load_x_transposes()
                rms_norm(0)
                ffn("g1", "u1", "d1")
            elif rt == 0:
                x1_load(0)
            rms_norm(16)
            if rt == 0:
                exchange_recv()
            attention(j, rt)
            conv_branch(False)
            merge_and_out()
            rms_norm(32)
            nxt_own = (rt + 1 < n_norm)
            if nxt_own:
                ffn("g2", "u2", "d2", after_gateup=lambda rt=rt: load_x_dma(rt + 1, (0, 1), "act"))
            else:
                ffn("g2", "u2", "d2")
            nxt_x1 = (rt + 1 < n_real) and not nxt_own
            store_x(rt, x1_next=(rt + 1 - n_norm) if nxt_x1 else None)
            if nxt_own:
                load_x_dma(rt + 1, (2, 3), "sp")

        P.assign_ticks()
        dma_sems = {}
        for k in dma_keys:
            if k not in dma_sems:
                dma_sems[k] = es.enter_context(nc.semaphore("d_" + str(len(dma_sems))))
        for k in P.dma_counts:
            assert k in dma_sems, k
        with nc.Block() as block:
            @block.sync
            def _(e):
                P.emit_queue("sp", e, eng_sems, dma_sems)

            @block.tensor
            def _(e):
                P.emit_queue("pe", e, eng_sems, dma_sems)

            @block.scalar
            def _(e):
                P.emit_queue("act", e, eng_sems, dma_sems)

            @block.vector
            def _(e):
                P.emit_queue("dve", e, eng_sems, dma_sems)

            @block.gpsimd
            def _(e):
                P.emit_queue("pool", e, eng_sems, dma_sems, final_dma_wait=True)
    return nc


def make_tabs():
    tabs = np.zeros((128, 576), np.float32)
    k = np.arange(128)[:, None]
    q = np.arange(256)[None, :]
    dist = q - k
    tabs[:, 0:256] = np.where((dist >= 0) & (dist <= 128), -dist, NEGM)
    for ph in range(5):
        p = np.arange(128)[:, None]
        qq = np.arange(32)[None, :]
        sl, klo = p // 32, p % 32
        dl = (ph - sl) % 5
        dist = 32 * dl + qq - klo
        tabs[:, 256 + 64 * ph: 256 + 64 * ph + 32] = np.where((dist >= 0) & (dist <= 128), -dist, NEGM)
        klo = np.arange(32)[:, None]
        dl = (ph - 4) % 5
        dist = 32 * dl + qq - klo
        tabs[0:32, 256 + 64 * ph + 32: 256 + 64 * ph + 64] = np.where((dist >= 0) & (dist <= 128), -dist, NEGM)
    return tabs


def make_kb(n_real, n_halo, seq_start):
    kb = np.zeros((128, 4 * n_real + 4), np.float32)
    kb[:, 4 * n_real] = 0.0 if seq_start else 1.0
    if not seq_start:
        return kb
    NEG = -30000.0
    for jr in range(n_real):
        j = n_halo + jr
        if j - 1 < n_halo:
            kb[:, 4 * jr + 0] = NEG
            kb[:, 4 * jr + 1] = NEG
        ph = j % 5
        p = np.arange(128)
        dl = (ph - p // 32) % 5
        kb[:, 4 * jr + 2] = np.where((dl > 0) & (j - dl < n_halo), NEG, 0.0)
        dl = (ph - 4) % 5
        if dl > 0 and j - dl < n_halo:
            kb[0:32, 4 * jr + 3] = NEG
    return kb


def make_vecs(ffn1_norm, mix_norm, ffn2_norm, q_norm, k_norm, conv_w):
    v = np.zeros((128, 120), np.float32)
    v[:, 0:16] = ffn1_norm.reshape(16, 128).T
    v[:, 16:32] = mix_norm.reshape(16, 128).T
    v[:, 32:48] = ffn2_norm.reshape(16, 128).T
    v[:, 48:60] = q_norm.reshape(12, 128).T
    v[:, 60:72] = k_norm.reshape(12, 128).T
    v[:, 72:120] = conv_w.reshape(3, 16, 128).transpose(2, 1, 0).reshape(128, 48)
    return v


_NC_CACHE = {}


def kernel(x, ffn1_norm, ffn1_w_gate, ffn1_w_up, ffn1_w_down, mix_norm, w_in,
           q_norm, k_norm, conv_w, w_attn_out, w_conv_out, w_o,
           ffn2_norm, ffn2_w_gate, ffn2_w_up, ffn2_w_down):
    x = np.asarray(x, np.float32)
    n_real, n_halo = TOK_CORE // T, HALO // T
    if "nc" not in _NC_CACHE:
        _NC_CACHE["nc"] = build_program(n_real, N_CORES)
    nc = _NC_CACHE["nc"]
    f = lambda a: np.ascontiguousarray(np.asarray(a, np.float32)[0])
    shared = {
        "ffn1_w_gate": f(ffn1_w_gate), "ffn1_w_up": f(ffn1_w_up), "ffn1_w_down": f(ffn1_w_down),
        "ffn2_w_gate": f(ffn2_w_gate), "ffn2_w_up": f(ffn2_w_up), "ffn2_w_down": f(ffn2_w_down),
        "w_in": f(w_in), "w_attn_out": f(w_attn_out), "w_conv_out": f(w_conv_out), "w_o": f(w_o),
        "vecs": make_vecs(f(ffn1_norm), f(mix_norm), f(ffn2_norm), f(q_norm), f(k_norm), f(conv_w)),
        "tabs": make_tabs(),
        "ident": np.eye(128, dtype=np.float32),
    }
    in_maps = []
    segs = SEQ // TOK_CORE
    for core in range(N_CORES):
        b, sgm = core // segs, core % segs
        s0 = sgm * TOK_CORE
        m = dict(shared)
        m["x"] = np.ascontiguousarray(x[b, s0:s0 + TOK_CORE])
        m["kb"] = make_kb(n_real, n_halo, sgm == 0)
        m["idx"] = np.array([[(core - 1) % N_CORES]], np.int32)
        in_maps.append(m)
    res = run_bass_kernel_spmd(nc, in_maps, core_ids=list(range(N_CORES)))
    out = np.zeros((BATCH, SEQ, D_MODEL), np.float32)
    for core in range(N_CORES):
        b, sgm = core // segs, core % segs
        out[b, sgm * TOK_CORE:(sgm + 1) * TOK_CORE] = res.results[core]["out"]
    return out
```

```python
import numpy as np
from contextlib import ExitStack
import concourse.bass as bass
import concourse.mybir as mybir
from concourse.bass_utils import run_bass_kernel_spmd

F32 = mybir.dt.float32
BF16 = mybir.dt.bfloat16
AF = mybir.ActivationFunctionType
ALU = mybir.AluOpType

D_MODEL = 2048
D_FF = 5632
NCH = D_MODEL // 128
NFF = D_FF // 128
NFH = NFF // 2
T = 512
SEQ = 16384
BATCH = 2
N_CORES = 8
TOK_CORE = BATCH * SEQ // N_CORES
HALO = 2048
RMS_EPS = 1e-6
NEGM = -1.0e6
DIL = (1, 4, 16)
SLOPES = [2.0 ** (-8.0 * (i + 1) / 12) for i in range(12)]
W_IN_COLS = 14848
CQ, CK, CV, CU, CGB, CGC, CGA, CGV = 0, 12, 24, 36, 52, 68, 84, 100

QUEUES = ("pe", "act", "dve", "pool", "sp")


class Op:
    __slots__ = ("q", "fn", "deps", "signal", "tick", "dma_key", "dma_val", "has_dependents")

    def __init__(self, q, fn):
        self.q = q
        self.fn = fn
        self.deps = []
        self.signal = False
        self.tick = 0
        self.dma_key = None
        self.dma_val = 0
        self.has_dependents = False


class Prog:
    def __init__(self, same_engine_sync=True):
        self.q = {k: [] for k in QUEUES}
        self.lastw = {}
        self.readers = {}
        self.dma_counts = {}
        self.same_engine_sync = same_engine_sync
        self.pending_bar = {}

    def barrier(self, key):
        for q in QUEUES:
            self.pending_bar[q] = key

    def add(self, q, fn, reads=(), writes=(), dma_key=None):
        op = Op(q, fn)
        if q in self.pending_bar:
            reads = list(reads) + [self.pending_bar.pop(q)]
        deps = {}
        for r in reads:
            w = self.lastw.get(r)
            if w is not None:
                deps[id(w)] = w
        for w_ in writes:
            w = self.lastw.get(w_)
            if w is not None:
                deps[id(w)] = w
            rd = self.readers.get(w_)
            if rd:
                for o in rd[0].values():
                    deps[id(o)] = o
                for o in rd[1]:
                    deps[id(o)] = o
        for d in deps.values():
            if d.dma_key is None and d.q == q:
                if q == "pe" or not self.same_engine_sync:
                    continue
            op.deps.append(d)
            d.has_dependents = True
        for r in reads:
            rd = self.readers.get(r)
            if rd is None:
                rd = ({}, [])
                self.readers[r] = rd
            if dma_key is not None:
                rd[1].append(op)
            else:
                rd[0][q] = op
        for w_ in writes:
            self.lastw[w_] = op
            self.readers[w_] = ({}, [])
        if dma_key is not None:
            op.dma_key = dma_key
            c = self.dma_counts.get(dma_key, 0) + 16
            self.dma_counts[dma_key] = c
            op.dma_val = c
        self.q[q].append(op)
        return op

    def assign_ticks(self):
        for qname in QUEUES:
            t = 0
            for op in self.q[qname]:
                if op.dma_key is None and op.has_dependents:
                    t += 1
                    op.signal = True
                    op.tick = t

    def emit_queue(self, qname, eng, eng_sems, dma_sems, final_dma_wait=False):
        known = {}
        for op in self.q[qname]:
            need = {}
            for d in op.deps:
                if d.dma_key is not None:
                    key = ("dma", d.dma_key)
                    val = d.dma_val
                    sem = dma_sems[d.dma_key]
                else:
                    key = ("q", d.q)
                    val = d.tick
                    sem = eng_sems[d.q]
                if known.get(key, 0) >= val:
                    continue
                if key not in need or need[key][1] < val:
                    need[key] = (sem, val)
            for key, (sem, val) in need.items():
                known[key] = val
                eng.wait_ge(sem, val)
            ins = op.fn(eng)
            if op.dma_key is not None:
                ins.then_inc(dma_sems[op.dma_key], 16)
            elif op.signal:
                ins.then_inc(eng_sems[qname], 1)
        if final_dma_wait:
            for key, cnt in self.dma_counts.items():
                if known.get(("dma", key), 0) < cnt:
                    eng.wait_ge(dma_sems[key], cnt)


def bc(ap, n):
    dims = list(ap.ap)
    return bass.AP(ap.tensor, ap.offset, [dims[0], (0, n)] + dims[1:])


def build_program(n_real=8, n_cores=8, same_engine_sync=True):
    n_halo = 4
    gsz = min(4, n_cores)
    groups = [list(range(g * gsz, (g + 1) * gsz)) for g in range(n_cores // gsz)]
    n_norm = n_real - n_halo
    assert n_norm >= 0
    nc = bass.Bass("TRN2", target_bir_lowering=False)
    P = Prog(same_engine_sync=same_engine_sync)

    def din(name, shape, dt=F32):
        return nc.dram_tensor(name, list(shape), dt, kind="ExternalInput").ap()

    x_d = din("x", [n_real * T, D_MODEL])
    wsrc = {
        "g1": din("ffn1_w_gate", [D_MODEL, D_FF]), "u1": din("ffn1_w_up", [D_MODEL, D_FF]),
        "d1": din("ffn1_w_down", [D_FF, D_MODEL]),
        "g2": din("ffn2_w_gate", [D_MODEL, D_FF]), "u2": din("ffn2_w_up", [D_MODEL, D_FF]),
        "d2": din("ffn2_w_down", [D_FF, D_MODEL]),
        "in": din("w_in", [D_MODEL, W_IN_COLS]),
        "ao": din("w_attn_out", [512, D_MODEL]),
        "co": din("w_conv_out", [D_MODEL, D_MODEL]),
        "o": din("w_o", [D_MODEL, D_MODEL]),
    }
    vecs_d = din("vecs", [128, 120])
    tabs_d = din("tabs", [128, 576])
    kb_d = din("kb", [128, 4 * n_real + 4])
    idx_d = din("idx", [1, 1], mybir.dt.int32)
    ident_d = din("ident", [128, 128])
    out_d = nc.dram_tensor("out", [n_real * T, D_MODEL], F32, kind="ExternalOutput").ap()
    x1s = nc.dram_tensor("x1s", [n_halo, 128, NCH * T], F32).ap()

    def dscr(name, nchunks, nk):
        return nc.dram_tensor(name, [nchunks, 128, nk * 128], BF16, kind="Internal").ap()

    wbf = {
        "g1": dscr("wbf_g1", NFF, NCH), "u1": dscr("wbf_u1", NFF, NCH),
        "d1": dscr("wbf_d1", NCH * 2, NFH),
        "g2": dscr("wbf_g2", NFF, NCH), "u2": dscr("wbf_u2", NFF, NCH),
        "d2": dscr("wbf_d2", NCH * 2, NFH),
        "in": dscr("wbf_in", 116, NCH), "ao": dscr("wbf_ao", NCH, 4),
        "co": dscr("wbf_co", NCH, NCH), "o": dscr("wbf_o", NCH, NCH),
    }

    es = ExitStack()
    with es:
        def sb(name, shape, dt):
            return es.enter_context(nc.sbuf_tensor("sb_" + name, list(shape), dt))

        xT = sb("xT", [128, NCH * T], F32)
        hT = sb("hT", [128, NCH * T], BF16)
        big = sb("big", [128, 32 * T], BF16)
        bigf = big.bitcast(F32)
        attnT = sb("attnT", [128, 4 * T], BF16)
        K1 = sb("K1", [128, 4 * 640], BF16)
        V1 = sb("V1", [128, 4 * 640], BF16)
        K2 = sb("K2", [128, 4 * 4 * 256], BF16)
        V2 = sb("V2", [128, 4 * 4 * 256], BF16)
        K3 = sb("K3", [128, 4 * 16 * 160], BF16)
        V3 = sb("V3", [128, 4 * 16 * 160], BF16)
        NWU = 7
        WU = 2048
        wbuf = sb("wbuf", [128, NWU * WU], BF16)
        NSCR = 8
        SCRW = 520
        scr = sb("scr", [128, NSCR * SCRW], F32)
        scrb = scr.bitcast(BF16)
        ident_f = sb("ident_f", [128, 128], F32)
        ident_b = sb("ident_b", [128, 128], BF16)
        ones_b = sb("ones_b", [128, 128], BF16)
        zeros_b = sb("zeros_b", [128, 128], BF16)
        vecs = sb("vecs", [128, 120], F32)
        tabs = sb("tabs", [128, 576], F32)
        kbt = sb("kbt", [128, 4 * n_real + 4], F32)
        idx_sb = sb("idx_sb", [1, 1], mybir.dt.int32)
        cc_sem = es.enter_context(nc.semaphore("cc_sem"))
        carry = sb("carry", [128, NCH * 2], F32)
        cst = sb("cst", [128, 4], F32)

        psb = [es.enter_context(nc.psum_tensor("ps%d" % i, [128, 512], F32)) for i in range(8)]
        psb_b = [p.bitcast(BF16) for p in psb]

        eng_sems = {q: es.enter_context(nc.semaphore("s_" + q)) for q in ("pe", "act", "dve", "pool")}
        dma_keys = []

        st = {"bank": 0, "scr": 0, "wu": 0, "abank": 0, "tile": 0}

        def bank(pool8=True):
            if pool8:
                b = st["bank"] % 8
                st["bank"] += 1
            else:
                b = st["abank"] % 4
                st["abank"] += 1
            return b

        def newscr():
            s = st["scr"] % NSCR
            st["scr"] += 1
            return s

        def scr_f(s, n=T, off=0):
            return scr[:, s * SCRW + off: s * SCRW + off + n]

        def scr_b(s, n=T, off=0):
            return scrb[:, s * 2 * SCRW + off: s * 2 * SCRW + off + n]

        def big_b(g0, n):
            return big[:, g0 * T: g0 * T + n]

        def gran(g0, nbytes):
            return [("big", g0 + i) for i in range((nbytes + 1023) // 1024)]

        def xc(c):
            return xT[:, c * T:(c + 1) * T]

        def hc(c):
            return hT[:, c * T:(c + 1) * T]

        def vcol(i):
            return vecs[:, i:i + 1]

        def wload(wname, chunk0, nchunks, nk):
            per = nk * 128
            units_per = (per + WU - 1) // WU
            nun = units_per * nchunks
            u0 = st["wu"] % NWU
            if u0 + nun > NWU:
                u0 = 0
            st["wu"] = u0 + nun
            keys = [("w", u0 + i) for i in range(nun)]
            dkey = ("w", u0)
            if dkey not in dma_keys:
                dma_keys.append(dkey)
            src = wbf[wname][chunk0:chunk0 + nchunks, :, :].rearrange("c p e -> p c e")
            if units_per * WU == per:
                dst = wbuf[:, u0 * WU:(u0 + nun) * WU].rearrange("p (c e) -> p c e", c=nchunks)
            else:
                dst = wbuf[:, u0 * WU:(u0 + nun) * WU].rearrange("p (c e) -> p c e", c=nchunks)[:, :, 0:per]
            chunks = list(range(chunk0, chunk0 + nchunks))
            if st["tile"] == 0 and all((wname, ch) not in cast_done for ch in chunks):
                skeys = []
                for i, ch in enumerate(chunks):
                    base = (u0 + i * units_per) * WU
                    for pi, (psrc, _pdst, k) in enumerate(cast_pieces(wname, ch, nk)):
                        off = base + (11 * 128 * pi if nk == NFH else 0)
                        lv = wbuf[:, off: off + k * 128].rearrange("p (k m) -> p k m", k=k)
                        uk = u0 + i * units_per + pi
                        dk = ("wd", uk)
                        if dk not in dma_keys:
                            dma_keys.append(dk)
                        P.add("pool", lambda e, lv=lv, psrc=psrc: e.dma_start(out=lv, in_=psrc), writes=[("w", uk)], dma_key=dk)
                    cast_done[(wname, ch)] = [("wbf", wname, ch)]
                    skeys.append(("wbf", wname, ch))
                sk = ("ws", u0)
                if sk not in dma_keys:
                    dma_keys.append(sk)
                P.add("sp", lambda e, dst=dst, src=src: e.dma_start(out=src, in_=dst), reads=keys, writes=skeys, dma_key=sk)
            else:
                ckeys = ensure_cast(wname, chunks, nk)
                P.add("sp", lambda e, dst=dst, src=src: e.dma_start(out=dst, in_=src), reads=ckeys, writes=keys, dma_key=dkey)
            aps = []
            for i in range(nchunks):
                base = (u0 + i * units_per) * WU
                aps.append(wbuf[:, base:base + per].rearrange("p (k m) -> p k m", k=nk))
            return aps, keys

        def mm_group(pb, wap, nk, rhs_fn, reads, col0=0, ncol=T):
            def fn(e, pb=pb, wap=wap, nk=nk, rhs_fn=rhs_fn):
                last = None
                for k in range(nk):
                    last = e.matmul(psb[pb][:, col0:col0 + ncol], wap[:, k, :], rhs_fn(k),
                                    start=(k == 0), stop=(k == nk - 1))
                return last
            P.add("pe", fn, reads=reads, writes=[("ps", pb)])

        def mm_multi(outs, nk, rhs_fn, rhs_key_fn, wkeys):
            for k in range(nk):
                def fn(e, k=k):
                    last = None
                    for pb, wap in outs:
                        last = e.matmul(psb[pb][:, :], wap[:, k, :], rhs_fn(k), start=(k == 0), stop=(k == nk - 1))
                    return last
                P.add("pe", fn, reads=[rhs_key_fn(k)] + wkeys, writes=[("ps", pb) for pb, _ in outs])

        def cload(dst, src, key):
            dma_keys.append(key)
            P.add("sp", lambda e: e.dma_start(out=dst, in_=src), writes=[key], dma_key=key)

        cload(ident_f[:], ident_d[:, :], "c_ident")
        cload(vecs[:], vecs_d[:, :], "c_vecs")
        cload(tabs[:], tabs_d[:, :], "c_tabs")
        cload(kbt[:], kb_d[:, :], "c_kb")
        cload(idx_sb[:], idx_d[:, :], "c_idx")
        P.add("pool", lambda e: e.tensor_copy(ident_b[:], ident_f[:]), reads=["c_ident"], writes=["ident_b"])
        P.add("pool", lambda e: e.memset(ones_b[:], 1.0), writes=["ones_b"])
        P.add("pool", lambda e: e.memset(zeros_b[:], 0.0), writes=["zeros_b"])
        P.add("pool", lambda e: e.memset(cst[:, 0:1], RMS_EPS), writes=["cst0"])
        P.add("pool", lambda e: e.memset(cst[:, 1:2], 128.0 * RMS_EPS), writes=["cst1"])
        P.add("pool", lambda e: e.memset(cst[:, 2:3], 0.0), writes=["cst2"])
        P.add("pool", lambda e: e.memset(carry[:], 0.0), writes=["carry"])
        for ci_, t_ in enumerate((K1, V1, K2, V2, K3, V3)):
            P.add("pool", lambda e, t_=t_: e.memset(t_[:], 0.0), writes=["cache%d" % ci_])
        CONST_READS = ["c_ident", "c_vecs", "c_tabs", "c_kb", "c_idx", "ident_b", "ones_b", "zeros_b", "cst0", "cst1", "cst2"]

        NLAND = 2
        land = sb("land", [128, NLAND * 2048], BF16)
        for k_ in range(NLAND):
            dma_keys.append(("ci", k_))
            dma_keys.append(("co", k_))
        cast_done = {}
        cstate = {"n": 0}
        BAR = CONST_READS + ["cache%d" % i for i in range(6)] + ["carry"]
        P.add("pool", lambda e: e.memset(cst[:, 3:4], 0.0), reads=BAR, writes=["bar"])
        P.barrier("bar")

        def cast_pieces(wname, chunk, nk):
            if nk == NFH:
                c, half = chunk // 2, chunk % 2
                out = []
                for q0 in (0, 11):
                    kk0 = half * NFH + q0
                    src = wsrc[wname].rearrange("(k p) n -> p k n", p=128)[:, kk0:kk0 + 11, c * 128:(c + 1) * 128]
                    dst = wbf[wname][chunk, :, q0 * 128:(q0 + 11) * 128]
                    out.append((src, dst, 11))
                return out
            src = wsrc[wname].rearrange("(k p) n -> p k n", p=128)[:, :, chunk * 128:(chunk + 1) * 128]
            return [(src, wbf[wname][chunk, :, :], nk)]

        def ensure_cast(wname, chunks, nk):
            keys = []
            todo = []
            for ch in chunks:
                if (wname, ch) not in cast_done:
                    pcs = cast_pieces(wname, ch, nk)
                    ks = [("wbf", wname, ch, i) for i in range(len(pcs))]
                    cast_done[(wname, ch)] = ks
                    for (src, dst, k), key in zip(pcs, ks):
                        todo.append((src, dst, k, key))
                keys += cast_done[(wname, ch)]
            for g0 in range(0, len(todo), NLAND):
                grp = todo[g0:g0 + NLAND]
                slots = []
                for (src, dst, k, key) in grp:
                    s_ = cstate["n"] % NLAND
                    cstate["n"] += 1
                    slots.append(s_)
                    lv = land[:, s_ * 2048: s_ * 2048 + k * 128].rearrange("p (k m) -> p k m", k=k)
                    P.add("pool", lambda e, lv=lv, src=src: e.dma_start(out=lv, in_=src), writes=[("land", s_)], dma_key=("ci", s_))
                for (src, dst, k, key), s_ in zip(grp, slots):
                    lv2 = land[:, s_ * 2048: s_ * 2048 + k * 128]
                    P.add("pool", lambda e, lv2=lv2, dst=dst: e.dma_start(out=dst, in_=lv2), reads=[("land", s_)], writes=[key], dma_key=("co", s_))
            return keys

        def RB(q):
            return []

        hTf = hT.bitcast(F32)

        def xstage(b):
            if b < 2:
                return hTf[:, b * 2048:(b + 1) * 2048], [("h", 8 * b + i) for i in range(8)]
            return bigf[:, (b - 2) * 2048:(b - 1) * 2048], gran(8 * (b - 2), 8192)

        def load_x_dma(j, blocks, q):
            for b in blocks:
                key = ("xl", b)
                if key not in dma_keys:
                    dma_keys.append(key)
                dst, keys = xstage(b)
                P.add(q, lambda e, b=b, dst=dst: e.dma_start(out=dst, in_=x_d[j * T + b * 128: j * T + (b + 1) * 128, :]),
                      writes=keys, dma_key=key)

        def load_x_transposes():
            for half in range(2):
                for c in range(NCH):
                    pb = bank()
                    rk = []
                    for b in (2 * half, 2 * half + 1):
                        rk += xstage(b)[1]
                    def fn(e, c=c, pb=pb, half=half):
                        last = None
                        for bb in range(2):
                            src = xstage(2 * half + bb)[0]
                            last = e.transpose(psb[pb][:, bb * 128:(bb + 1) * 128], src[:, c * 128:(c + 1) * 128], ident_f[:])
                        return last
                    P.add("pe", fn, reads=rk, writes=[("ps", pb)])
                    dst = xT[:, c * T + half * 256: c * T + (half + 1) * 256]
                    if c % 2 == 0:
                        P.add("act", lambda e, dst=dst, pb=pb: e.copy(dst, psb[pb][:, 0:256]), reads=[("ps", pb)], writes=[("x", c)])
                    else:
                        P.add("dve", lambda e, dst=dst, pb=pb: e.tensor_copy(dst, psb[pb][:, 0:256]), reads=[("ps", pb)], writes=[("x", c)])

        def store_x(jr, x1_next=None):
            stage = bigf
            order = [(b, g4) for b in range(4) for g4 in range(4)] if x1_next is None else \
                    [(b, g4) for g4 in range(4) for b in range(4)]
            for n_, (b, g4) in enumerate(order):
                pb = bank()
                def fn(e, b=b, g4=g4, pb=pb):
                    last = None
                    for cc in range(4):
                        c = g4 * 4 + cc
                        last = e.transpose(psb[pb][:, cc * 128:(cc + 1) * 128],
                                           xT[:, c * T + b * 128: c * T + (b + 1) * 128], ident_f[:])
                    return last
                P.add("pe", fn, reads=[("x", g4 * 4 + cc) for cc in range(4)], writes=[("ps", pb)])
                dst = stage[:, b * 2048 + g4 * 512: b * 2048 + (g4 + 1) * 512]
                grs = gran(8 * b + 2 * g4, 2048)
                if n_ % 2 == 0:
                    P.add("act", lambda e, dst=dst, pb=pb: e.copy(dst, psb[pb][:, :]), reads=[("ps", pb)], writes=grs)
                else:
                    P.add("dve", lambda e, dst=dst, pb=pb: e.tensor_copy(dst, psb[pb][:, :]), reads=[("ps", pb)], writes=grs)
                if x1_next is None and g4 == 3:
                    store_dma(jr, b)
                if x1_next is not None and b == 3:
                    x1_load(x1_next, (g4,))
            if x1_next is not None:
                for b in range(4):
                    store_dma(jr, b)

        def store_dma(jr, b):
            key = ("xs", b)
            if key not in dma_keys:
                dma_keys.append(key)
            P.add("act", lambda e: e.dma_start(out=out_d[jr * T + b * 128: jr * T + (b + 1) * 128, :],
                                                in_=bigf[:, b * 2048:(b + 1) * 2048]),
                  reads=gran(8 * b, 8192), dma_key=key)

        def rms_norm(gain_base):
            pb = bank()
            sq = []
            for c in range(NCH):
                s = newscr()
                sq.append(s)
                P.add("act", lambda e, c=c, s=s: e.activation(scr_b(s), xc(c), AF.Square),
                      reads=[("x", c)], writes=[("scr", s)])
                P.add("pe", lambda e, c=c, s=s, pb=pb: e.matmul(psb[pb][:, :], ones_b[:], scr_b(s),
                                                                 start=(c == 0), stop=(c == NCH - 1)),
                      reads=[("scr", s)], writes=[("ps", pb)])
            s1 = newscr()
            P.add("act", lambda e: e.activation(scr_f(s1), psb[pb][:, :], AF.Ln, bias=cst[:, 0:1], scale=1.0 / D_MODEL),
                  reads=[("ps", pb)], writes=[("scr", s1)])
            P.add("act", lambda e: e.activation(scr_f(s1), scr_f(s1), AF.Exp, scale=-0.5), reads=[("scr", s1)], writes=[("scr", s1)])
            for c in range(NCH):
                P.add("dve", lambda e, c=c: e.scalar_tensor_tensor(hc(c), xc(c), vcol(gain_base + c), scr_f(s1), ALU.mult, ALU.mult),
                      reads=[("x", c), ("scr", s1)], writes=[("h", c)])

        def ffn(wg, wu, wd, after_gateup=None, on_chunk=None):
            HREADS = [("h", c) for c in range(NCH)]
            for half in range(2):
                for i in range(NFH // 2):
                    bi = half * NFH + 2 * i
                    gaps, gk = wload(wg, bi, 2, NCH)
                    uaps, uk = wload(wu, bi, 2, NCH)
                    first = (half == 0 and i == 0)
                    if first:
                        fb = [bank() for _ in range(4)]
                        mm_multi([(fb[0], gaps[0]), (fb[1], uaps[0]), (fb[2], gaps[1]), (fb[3], uaps[1])], NCH,
                                 lambda k: hc(k), lambda k: ("h", k), gk + uk)
                    for sub in range(2):
                        m = 2 * i + sub
                        if first:
                            pg, pu = fb[2 * sub], fb[2 * sub + 1]
                        else:
                            pg = bank()
                            mm_group(pg, gaps[sub], NCH, lambda k: hc(k), HREADS + gk)
                            pu = bank()
                            mm_group(pu, uaps[sub], NCH, lambda k: hc(k), HREADS + uk)
                        s = newscr()
                        P.add("act", lambda e, s=s, pg=pg: e.activation(scr_f(s), psb[pg][:, :], AF.Silu),
                              reads=[("ps", pg)], writes=[("scr", s)])
                        P.add("dve", lambda e, s=s, pu=pu, m=m: e.tensor_tensor(big_b(m, T), scr_f(s), psb[pu][:, :], ALU.mult),
                              reads=[("scr", s), ("ps", pu)], writes=[("big", m)])
                if half == 1 and after_gateup is not None:
                    after_gateup()
                for c in range(NCH):
                    daps, dk = wload(wd, c * 2 + half, 1, NFH)
                    pd = bank()
                    mm_group(pd, daps[0], NFH, lambda k: big_b(k, T), [("big", m) for m in range(NFH)] + dk)
                    P.add("dve", lambda e, c=c, pd=pd: e.scalar_tensor_tensor(xc(c), psb[pd][:, :], 0.5, xc(c), ALU.mult, ALU.add),
                          reads=[("ps", pd), ("x", c)], writes=[("x", c)])
                    if half == 1 and on_chunk is not None:
                        on_chunk(c)

        def kv_dst(cache, g, hh, j):
            if g == 0:
                return cache[:, hh * 640 + 128: hh * 640 + 640]
            if g == 1:
                s2 = j % 2
                v = cache[:, hh * 1024:(hh + 1) * 1024].rearrange("p (r s) -> p r s", r=4)
                return v[:, :, s2 * 128:(s2 + 1) * 128]
            s5 = j % 5
            v = cache[:, hh * 2560:(hh + 1) * 2560].rearrange("p (r s) -> p r s", r=16)
            return v[:, :, s5 * 32:(s5 + 1) * 32]

        def perm(ap, g):
            if g == 0:
                return ap
            return ap.rearrange("p (i r) -> p r i", r=DIL[g])

        CACHES = ((K1, V1), (K2, V2), (K3, V3))

        def head_post(pk, gain_col, is_q, dst, dst_keys, g, bankfn):
            s = newscr()
            P.add("act", lambda e: e.activation(scr_b(s), psb[pk][:, :], AF.Square), reads=[("ps", pk)], writes=[("scr", s)])
            pss = bankfn()
            P.add("pe", lambda e: e.matmul(psb[pss][:, :], ones_b[:], scr_b(s), start=True, stop=True),
                  reads=[("scr", s)], writes=[("ps", pss)])
            s2 = newscr()
            if is_q:
                P.add("act", lambda e: e.activation(scr_f(s2), psb[pss][:, :], AF.Ln, bias=cst[:, 1:2], scale=1.0),
                      reads=[("ps", pss)], writes=[("scr", s2)])
            else:
                P.add("act", lambda e: e.activation(scr_f(s2), psb[pss][:, :], AF.Ln, bias=cst[:, 0:1], scale=1.0 / 128.0),
                      reads=[("ps", pss)], writes=[("scr", s2)])
            P.add("act", lambda e: e.activation(scr_f(s2), scr_f(s2), AF.Exp, scale=-0.5), reads=[("scr", s2)], writes=[("scr", s2)])
            P.add("dve", lambda e: e.scalar_tensor_tensor(dst, perm(psb[pk][:, :], g), gain_col, perm(scr_f(s2), g), ALU.mult, ALU.mult),
                  reads=[("ps", pk), ("scr", s2)], writes=dst_keys)

        def kvq(j, hh, with_q, qset, bankfn, first, groups=(0, 1, 2)):
            HREADS = [("h", c) for c in range(NCH)]
            items = []
            for g in groups:
                aps, wk = wload("in", CK + 4 * g + hh, 1, NCH)
                items.append(("k", g, aps[0], wk))
            if with_q:
                for g in range(3):
                    aps, wk = wload("in", CQ + 4 * g + hh, 1, NCH)
                    items.append(("q", g, aps[0], wk))
            banks = []
            nfirst = 4 if (first and with_q) else (len(groups) if first else 0)
            if nfirst:
                fb = [bankfn() for _ in range(nfirst)]
                allk = []
                for it in items[:nfirst]:
                    allk += it[3]
                mm_multi([(fb[i], items[i][2]) for i in range(nfirst)], NCH, lambda k: hc(k), lambda k: ("h", k), allk)
            for idx, (kind, g, wap, wk) in enumerate(items):
                if idx < nfirst:
                    pk = fb[idx]
                else:
                    pk = bankfn()
                    mm_group(pk, wap, NCH, lambda k: hc(k), HREADS + wk)
                hd = 4 * g + hh
                if kind == "k":
                    head_post(pk, vcol(60 + hd), False, kv_dst(CACHES[g][0], g, hh, j), [("K", g, hh)], g, bankfn)
                else:
                    dst = big_b(qset * 3 + g, T)
                    if g > 0:
                        dst = dst.rearrange("p (r i) -> p r i", r=DIL[g])
                    head_post(pk, vcol(48 + hd), True, dst, [("big", qset * 3 + g)], g, bankfn)
            for g in groups:
                aps, wk = wload("in", CV + 4 * g + hh, 1, NCH)
                pv = bankfn()
                mm_group(pv, aps[0], NCH, lambda k: hc(k), HREADS + wk)
                P.add("act", lambda e, pv=pv, g=g: e.copy(kv_dst(CACHES[g][1], g, hh, j), perm(psb[pv][:, :], g)),
                      reads=[("ps", pv)], writes=[("V", g, hh)])

        def attention(j, jr):
            s2 = j % 2
            ph = j % 5
            VS1 = 12 * T
            VS2 = VS1 + 5 * 128
            VS3X = VS2 + 8 * 128
            VS3Y = VS3X + 16 * 128
            VSKEYS = [("big", i) for i in range(12, 24)]
            PB = 6 * T
            PG = 6
            def pbuf(i):
                return big[:, PB + i * T: PB + (i + 1) * T]
            def attn_head(hh):
                qs = hh % 2
                k1 = K1[:, hh * 640:(hh + 1) * 640]
                v1 = V1[:, hh * 640:(hh + 1) * 640]
                k2 = K2[:, hh * 1024:(hh + 1) * 1024].rearrange("p (r s) -> p r s", r=4)
                v2 = V2[:, hh * 1024:(hh + 1) * 1024].rearrange("p (r s) -> p r s", r=4)
                k3 = K3[:, hh * 2560:(hh + 1) * 2560].rearrange("p (r s) -> p r s", r=16)
                v3 = V3[:, hh * 2560:(hh + 1) * 2560].rearrange("p (r s) -> p r s", r=16)
                q1 = big_b(qs * 3 + 0, T)
                q2 = big_b(qs * 3 + 1, T).rearrange("p (r i) -> p r i", r=4)
                q3 = big_b(qs * 3 + 2, T).rearrange("p (r i) -> p r i", r=16)
                KR = [[("K", g, hh)] for g in range(3)]
                VR = [[("V", g, hh)] for g in range(3)]
                sb_ = [bank(False) for _ in range(6)]
                def s_g1(e, kind, pb):
                    last = None
                    for c in range(4):
                        kb0 = 128 * c if kind == 0 else 128 * (c + 1)
                        last = e.matmul(psb[pb][:, c * 128:(c + 1) * 128], k1[:, kb0:kb0 + 128], q1[:, c * 128:(c + 1) * 128],
                                        start=True, stop=True)
                    return last
                def s_g2(e, kind, pb):
                    last = None
                    so = (1 - s2) if kind == 0 else s2
                    for r in range(4):
                        last = e.matmul(psb[pb][:, r * 128:(r + 1) * 128], k2[:, r, so * 128:(so + 1) * 128], q2[:, r, :],
                                        start=True, stop=True)
                    return last
                def s_g3(e, kind, pb):
                    last = None
                    for r in range(16):
                        if kind == 0:
                            last = e.matmul(psb[pb][:, r * 32:(r + 1) * 32], k3[:, r, 0:128], q3[:, r, :], start=True, stop=True)
                        else:
                            last = e.matmul(psb[pb][0:32, r * 32:(r + 1) * 32], k3[:, r, 128:160], q3[:, r, :], start=True, stop=True)
                    return last
                sfn = [lambda e, pb=sb_[0]: s_g1(e, 0, pb), lambda e, pb=sb_[1]: s_g1(e, 1, pb),
                       lambda e, pb=sb_[2]: s_g2(e, 0, pb), lambda e, pb=sb_[3]: s_g2(e, 1, pb),
                       lambda e, pb=sb_[4]: s_g3(e, 0, pb), lambda e, pb=sb_[5]: s_g3(e, 1, pb)]
                sg = [0, 0, 1, 1, 2, 2]
                qg = [qs * 3, qs * 3, qs * 3 + 1, qs * 3 + 1, qs * 3 + 2, qs * 3 + 2]
                Dt = tabs
                tviews = [bc(Dt[:, 128:256], 4), bc(Dt[:, 0:128], 4), bc(Dt[:, 128:256], 4), bc(Dt[:, 0:128], 4),
                          bc(Dt[:, 256 + 64 * ph: 256 + 64 * ph + 32], 16), bc(Dt[0:32, 256 + 64 * ph + 32: 256 + 64 * ph + 64], 16)]
                shp = [(128, 4), (128, 4), (128, 4), (128, 4), (128, 16), (32, 16)]
                for i in range(6):
                    g = sg[i]
                    cval = SLOPES[4 * g + hh] * DIL[g]
                    P.add("pe", sfn[i], reads=KR[g] + [("big", qg[i])], writes=[("ps", sb_[i])])
                    s = newscr()
                    np_, nu = shp[i]
                    tmp = scr_f(s)[0:np_, :].rearrange("p (u n) -> p u n", u=nu)
                    sin = psb[sb_[i]][0:np_, :].rearrange("p (u n) -> p u n", u=nu)
                    P.add("dve", lambda e, tmp=tmp, sin=sin, tv=tviews[i], cval=cval: e.scalar_tensor_tensor(tmp, tv, cval, sin, ALU.mult, ALU.add),
                          reads=[("ps", sb_[i])], writes=[("scr", s)])
                    pdst = pbuf(i)
                    pkey = [("big", PG + i)]
                    if i == 0:
                        P.add("act", lambda e, s=s, pdst=pdst: e.activation(pdst[:, 0:128], scr_f(s)[:, 0:128], AF.Exp, bias=kbt[:, 4 * jr: 4 * jr + 1]),
                              reads=[("scr", s)], writes=pkey)
                        P.add("act", lambda e, s=s, pdst=pdst: e.activation(pdst[:, 128:512], scr_f(s)[:, 128:512], AF.Exp),
                              reads=[("scr", s)], writes=[("big", PG + i, "b")])
                    elif i == 2:
                        P.add("act", lambda e, s=s, pdst=pdst: e.activation(pdst[:, :], scr_f(s), AF.Exp, bias=kbt[:, 4 * jr + 1: 4 * jr + 2]),
                              reads=[("scr", s)], writes=pkey)
                    elif i == 4:
                        P.add("act", lambda e, s=s, pdst=pdst: e.activation(pdst[:, :], scr_f(s), AF.Exp, bias=kbt[:, 4 * jr + 2: 4 * jr + 3]),
                              reads=[("scr", s)], writes=pkey)
                    elif i == 5:
                        P.add("act", lambda e, s=s, pdst=pdst: e.activation(pdst[0:32, :], scr_f(s)[0:32, :], AF.Exp, bias=kbt[0:32, 4 * jr + 3: 4 * jr + 4]),
                              reads=[("scr", s)], writes=pkey)
                    else:
                        P.add("act", lambda e, s=s, pdst=pdst: e.activation(pdst[:, :], scr_f(s), AF.Exp),
                              reads=[("scr", s)], writes=pkey)
                def tr_batch(srcs, np_, dst_off, evq):
                    for b0 in range(0, len(srcs), 8):
                        grp = srcs[b0:b0 + 8]
                        pb = bank(False)
                        def fn(e, grp=grp, pb=pb):
                            last = None
                            for t_, sap in enumerate(grp):
                                last = e.transpose(psb_b[pb][0:np_, t_ * 128:(t_ + 1) * 128], sap, ident_b[:])
                            return last
                        P.add("pe", fn, reads=VR[0] + VR[1] + VR[2], writes=[("ps", pb)])
                        n = len(grp) * 128
                        dst = big[0:np_, dst_off + b0 * 128: dst_off + b0 * 128 + n]
                        if evq == "act":
                            P.add("act", lambda e, dst=dst, pb=pb, n=n: e.copy(dst, psb_b[pb][0:np_, 0:n]), reads=[("ps", pb)], writes=VSKEYS)
                        else:
                            P.add("dve", lambda e, dst=dst, pb=pb, n=n: e.tensor_copy(dst, psb_b[pb][0:np_, 0:n]), reads=[("ps", pb)], writes=VSKEYS)
                tr_batch([v1[:, 128 * b:128 * (b + 1)] for b in range(5)], 128, VS1, "act")
                tr_batch([v2[:, r, so * 128:(so + 1) * 128] for r in range(4) for so in range(2)], 128, VS2, "act")
                tr_batch([v3[:, r, 0:128] for r in range(16)], 128, VS3X, "act")
                tr_batch([v3[:, r, 128:160] for r in range(16)], 32, VS3Y, "act")
                return lambda: attn_head_b(hh)

            def attn_head_b(hh):
                VS1_, VS2_, VS3X_, VS3Y_ = VS1, VS2, VS3X, VS3Y
                pN = 4 + bank(False)
                pL = 4 + bank(False)
                P.add("pe", lambda e, pN=pN: e.matmul(psb[pN][:, :], zeros_b[:], hc(0), start=True, stop=False),
                      reads=[("h", 0)], writes=[("ps", pN)])
                P.add("pe", lambda e, pL=pL: e.matmul(psb[pL][:, :], zeros_b[:], hc(0), start=True, stop=False),
                      reads=[("h", 0)], writes=[("ps", pL)])
                pN_g2 = psb[pN][:, :].rearrange("p (i r) -> p r i", r=4)
                pL_g2 = psb[pL][:, :].rearrange("p (i r) -> p r i", r=4)
                pN_g3 = psb[pN][:, :].rearrange("p (i r) -> p r i", r=16)
                pL_g3 = psb[pL][:, :].rearrange("p (i r) -> p r i", r=16)
                def pv_fn(e, pN=pN, pL=pL, pN_g2=pN_g2, pL_g2=pL_g2, pN_g3=pN_g3, pL_g3=pL_g3):
                    vs1 = big[:, VS1:VS1 + 640].rearrange("p (b d) -> p b d", b=5)
                    vs2 = big[:, VS2:VS2 + 1024].rearrange("p (b d) -> p b d", b=8)
                    vs3x = big[:, VS3X:VS3X + 2048].rearrange("p (b d) -> p b d", b=16)
                    vs3y = big[0:32, VS3Y:VS3Y + 2048].rearrange("p (b d) -> p b d", b=16)
                    e.matmul(psb[pL][:, :], ones_b[:], pbuf(0)[:, :], start=False, stop=False)
                    e.matmul(psb[pL][:, :], ones_b[:], pbuf(1)[:, :], start=False, stop=False)
                    for kind in range(2):
                        e.matmul(pL_g2, ones_b[:], pbuf(2 + kind)[:, :].rearrange("p (r i) -> p r i", r=4), start=False, stop=False)
                    e.matmul(pL_g3, ones_b[:], pbuf(4)[:, :].rearrange("p (r i) -> p r i", r=16), start=False, stop=False)
                    e.matmul(pL_g3, ones_b[0:32, :], pbuf(5)[0:32, :].rearrange("p (r i) -> p r i", r=16), start=False, stop=True)
                    for c in range(4):
                        for kind in range(2):
                            pp = pbuf(kind)[:, c * 128:(c + 1) * 128]
                            vb = vs1[:, c + kind, :]
                            e.matmul(psb[pN][:, c * 128:(c + 1) * 128], vb, pp, start=False, stop=False)
                    for r in range(4):
                        for kind in range(2):
                            so = (1 - s2) if kind == 0 else s2
                            pp = pbuf(2 + kind)[:, r * 128:(r + 1) * 128]
                            vb = vs2[:, r * 2 + so, :]
                            e.matmul(pN_g2[:, r, :], vb, pp, start=False, stop=False)
                    last = None
                    for r in range(16):
                        pp = pbuf(4)[:, r * 32:(r + 1) * 32]
                        e.matmul(pN_g3[:, r, :], vs3x[:, r, :], pp, start=False, stop=False)
                        pp = pbuf(5)[0:32, r * 32:(r + 1) * 32]
                        last = e.matmul(pN_g3[:, r, :], vs3y[:, r, :], pp, start=False, stop=(r == 15))
                    return last
                P.add("pe", pv_fn, reads=VSKEYS + [("big", PG + i) for i in range(6)] + [("big", PG, "b")],
                      writes=[("ps", pN), ("ps", pL)])
                s = newscr()
                P.add("dve", lambda e, s=s, pL=pL: e.reciprocal(scr_f(s), psb[pL][:, :]), reads=[("ps", pL)], writes=[("scr", s)])
                P.add("dve", lambda e, s=s, pN=pN, hh=hh: e.tensor_tensor(attnT[:, hh * T:(hh + 1) * T], psb[pN][:, :], scr_f(s), ALU.mult),
                      reads=[("ps", pN), ("scr", s)], writes=[("attnT", hh)])
            kvq(j, 0, True, 0, bank, True)
            pend = None
            for hh in range(4):
                partb = attn_head(hh)
                if hh < 3:
                    kvq(j, hh + 1, True, (hh + 1) % 2, lambda: bank(False), False)
                partb()
            for hh in range(4):
                P.add("dve", lambda e, hh=hh: e.tensor_copy(K1[:, hh * 640: hh * 640 + 128], K1[:, hh * 640 + 512: hh * 640 + 640]),
                      reads=[("K", 0, hh)], writes=[("K", 0, hh)])
                P.add("dve", lambda e, hh=hh: e.tensor_copy(V1[:, hh * 640: hh * 640 + 128], V1[:, hh * 640 + 512: hh * 640 + 640]),
                      reads=[("V", 0, hh)], writes=[("V", 0, hh)])

        def roll_g1_only():
            for hh in range(4):
                P.add("dve", lambda e, hh=hh: e.tensor_copy(K1[:, hh * 640: hh * 640 + 128], K1[:, hh * 640 + 512: hh * 640 + 640]),
                      reads=[("K", 0, hh)] + RB("pool"), writes=[("K", 0, hh)])
                P.add("dve", lambda e, hh=hh: e.tensor_copy(V1[:, hh * 640: hh * 640 + 128], V1[:, hh * 640 + 512: hh * 640 + 640]),
                      reads=[("V", 0, hh)], writes=[("V", 0, hh)])

        def conv_branch(state_only):
            HREADS = [("h", c) for c in range(NCH)]
            for c in range(NCH):
                uaps, uk = wload("in", CU + c, 1, NCH)
                caps, ck = wload("in", CGC + c, 1, NCH)
                pu = bank()
                mm_group(pu, uaps[0], NCH, lambda k: hc(k), HREADS + uk)
                pc = bank()
                mm_group(pc, caps[0], NCH, lambda k: hc(k), HREADS + ck)
                if not state_only:
                    baps, bk = wload("in", CGB + c, 1, NCH)
                    pbk = bank()
                    mm_group(pbk, baps[0], NCH, lambda k: hc(k), HREADS + bk)
                su = newscr()
                P.add("act", lambda e, su=su, pu=pu: e.copy(scr_f(su), psb[pu][:, :]), reads=[("ps", pu)], writes=[("scr", su)])
                sz = newscr()
                P.add("dve", lambda e, sz=sz, c=c: e.tensor_copy(scr_f(sz, 2, 0), carry[:, 2 * c: 2 * c + 2]),
                      reads=["carry"], writes=[("scr", sz)])
                P.add("dve", lambda e, sz=sz, su=su, pc=pc: e.tensor_tensor(scr_f(sz, T, 2), psb[pc][:, :], scr_f(su), ALU.mult),
                      reads=[("ps", pc), ("scr", su), ("scr", sz)], writes=[("scr", sz)])
                P.add("dve", lambda e, sz=sz, c=c: e.tensor_copy(carry[:, 2 * c: 2 * c + 2], scr_f(sz, 2, T)),
                      reads=[("scr", sz)], writes=["carry"])
                if state_only:
                    continue
                sa = newscr()
                w0, w1, w2 = vcol(72 + 3 * c), vcol(72 + 3 * c + 1), vcol(72 + 3 * c + 2)
                P.add("dve", lambda e, sa=sa, sz=sz, w0=w0: e.tensor_scalar(scr_f(sa), scr_f(sz, T, 0), w0, None, ALU.mult),
                      reads=[("scr", sz)], writes=[("scr", sa)])
                P.add("dve", lambda e, sa=sa, sz=sz, w1=w1: e.scalar_tensor_tensor(scr_f(sa), scr_f(sz, T, 1), w1, scr_f(sa), ALU.mult, ALU.add),
                      reads=[("scr", sz), ("scr", sa)], writes=[("scr", sa)])
                P.add("dve", lambda e, sa=sa, sz=sz, w2=w2: e.scalar_tensor_tensor(scr_f(sa), scr_f(sz, T, 2), w2, scr_f(sa), ALU.mult, ALU.add),
                      reads=[("scr", sz), ("scr", sa)], writes=[("scr", sa)])
                P.add("dve", lambda e, sa=sa, pbk=pbk, c=c: e.tensor_tensor(big_b(c, T), psb[pbk][:, :], scr_f(sa), ALU.mult),
                      reads=[("ps", pbk), ("scr", sa)], writes=[("big", c)])

        def merge_and_out():
            HREADS = [("h", c) for c in range(NCH)]
            for c in range(NCH):
                aaps, ak = wload("ao", c, 1, 4)
                coaps, cok = wload("co", c, 1, NCH)
                gaaps, gak = wload("in", CGA + c, 1, NCH)
                gvaps, gvk = wload("in", CGV + c, 1, NCH)
                pA = bank()
                mm_group(pA, aaps[0], 4, lambda k: attnT[:, k * T:(k + 1) * T], [("attnT", k) for k in range(4)] + ak)
                pB = bank()
                mm_group(pB, coaps[0], NCH, lambda k: big_b(k, T), [("big", k) for k in range(NCH)] + cok)
                pGa = bank()
                mm_group(pGa, gaaps[0], NCH, lambda k: hc(k), HREADS + gak)
                pGv = bank()
                mm_group(pGv, gvaps[0], NCH, lambda k: hc(k), HREADS + gvk)
                s1, s2_ = newscr(), newscr()
                P.add("act", lambda e, s1=s1, pGa=pGa: e.activation(scr_f(s1), psb[pGa][:, :], AF.Sigmoid), reads=[("ps", pGa)], writes=[("scr", s1)])
                P.add("act", lambda e, s2_=s2_, pGv=pGv: e.activation(scr_f(s2_), psb[pGv][:, :], AF.Sigmoid), reads=[("ps", pGv)], writes=[("scr", s2_)])
                P.add("dve", lambda e, s1=s1, pA=pA: e.tensor_tensor(scr_f(s1), psb[pA][:, :], scr_f(s1), ALU.mult),
                      reads=[("ps", pA), ("scr", s1)], writes=[("scr", s1)])
                P.add("dve", lambda e, s2_=s2_, pB=pB: e.tensor_tensor(scr_f(s2_), psb[pB][:, :], scr_f(s2_), ALU.mult),
                      reads=[("ps", pB), ("scr", s2_)], writes=[("scr", s2_)])
                P.add("dve", lambda e, s1=s1, s2_=s2_, c=c: e.tensor_tensor(big_b(16 + c, T), scr_f(s1), scr_f(s2_), ALU.add),
                      reads=[("scr", s1), ("scr", s2_)], writes=[("big", 16 + c)])
            for c in range(NCH):
                oaps, ok = wload("o", c, 1, NCH)
                po = bank()
                mm_group(po, oaps[0], NCH, lambda k: big_b(16 + k, T), [("big", 16 + k) for k in range(NCH)] + ok)
                P.add("dve", lambda e, c=c, po=po: e.tensor_tensor(xc(c), psb[po][:, :], xc(c), ALU.add),
                      reads=[("ps", po), ("x", c)], writes=[("x", c)])

        def x1_store(t, c):
            if c % 4 != 3:
                return
            g4 = c // 4
            key = ("x1st", g4)
            if key not in dma_keys:
                dma_keys.append(key)
            P.add("act", lambda e: e.dma_start(out=x1s[t, :, g4 * 4 * T:(g4 + 1) * 4 * T], in_=xT[:, g4 * 4 * T:(g4 + 1) * 4 * T]),
                  reads=[("x", g4 * 4 + i) for i in range(4)], writes=[("x1s", t, g4)], dma_key=key)

        def x1_load(t, groups=(0, 1, 2, 3)):
            for g4 in groups:
                key = ("x1ld", g4)
                if key not in dma_keys:
                    dma_keys.append(key)
                P.add("act", lambda e, g4=g4: e.dma_start(out=xT[:, g4 * 4 * T:(g4 + 1) * 4 * T], in_=x1s[t, :, g4 * 4 * T:(g4 + 1) * 4 * T]),
                      reads=[("x1s", t, g4)], writes=[("x", g4 * 4 + i) for i in range(4)], dma_key=key)

        def kv_keys(kind, g):
            return [(kind, g, hh) for hh in range(4)]
        carry_b = carry.bitcast(BF16)
        k2s = K2[:, :].rearrange("p (a s) -> p a s", s=256)[:, :, 128:256]
        v2s = V2[:, :].rearrange("p (a s) -> p a s", s=256)[:, :, 128:256]
        k1s = K1[:, :].rearrange("p (a s) -> p a s", s=640)[:, :, 0:128]
        v1s = V1[:, :].rearrange("p (a s) -> p a s", s=640)[:, :, 0:128]
        BUFS = [
            [(K3[:, 0:4096], 0, 4096, 0, kv_keys("K", 2))],
            [(K3[:, 4096:8192], 0, 4096, 0, kv_keys("K", 2))],
            [(K3[:, 8192:10240], 0, 2048, 0, kv_keys("K", 2)), (V3[:, 0:2048], 2048, 2048, 0, kv_keys("V", 2))],
            [(V3[:, 2048:6144], 0, 4096, 0, kv_keys("V", 2))],
            [(V3[:, 6144:10240], 0, 4096, 0, kv_keys("V", 2))],
            [(k2s, 0, 2048, 16, kv_keys("K", 1)), (v2s, 2048, 2048, 16, kv_keys("V", 1))],
            [(k1s, 0, 512, 4, kv_keys("K", 0)), (v1s, 512, 512, 4, kv_keys("V", 0)), (carry_b[:, :], 1024, 64, 0, ["carry"])],
        ]
        BW = [sum(p[2] for p in bf) for bf in BUFS]
        sendbs = [nc.dram_tensor("sendb%d" % i, [128, BW[i]], BF16) for i in range(len(BUFS))]
        recvbs = [nc.dram_tensor("recvb%d" % i, [gsz * 128, BW[i]], BF16) for i in range(len(BUFS))]

        def dview(ap2d, a):
            return ap2d if a == 0 else ap2d.rearrange("p (a b) -> p a b", a=a)

        def exchange_send():
            keys = []
            n_ = 0
            for i, bf in enumerate(BUFS):
                for (sap, o, n, a, ks) in bf:
                    key = ("snd", n_)
                    n_ += 1
                    dma_keys.append(key)
                    P.add("sp", lambda e, sap=sap, o=o, n=n, a=a, i=i: e.dma_start(out=dview(sendbs[i].ap()[:, o:o + n], a), in_=sap),
                          reads=ks, writes=[("sendb", key)], dma_key=key)
                    keys.append(("sendb", key))
            def coll_fn(g_):
                for i in range(len(BUFS)):
                    ins = g_.collective_compute("AllGather", ALU.bypass, replica_groups=groups,
                                                ins=[sendbs[i].ap().opt()], outs=[recvbs[i].ap().opt()])
                    ins.then_inc(cc_sem)
                    g_.wait_ge(cc_sem, i + 1)
                return g_.memset(cst[:, 3:4], 0.0)
            P.add("pool", coll_fn, reads=keys, writes=["recvb"])

        xs_ = {}

        def exchange_recv():
            n_ = 0
            for i, bf in enumerate(BUFS):
                for (sap, o, n, a, ks) in bf:
                    key = ("rcv", n_)
                    dma_keys.append(key)
                    def fn(e, sap=sap, o=o, n=n, a=a, i=i, first=(n_ == 0)):
                        if first:
                            reg = e.alloc_register("ridx")
                            e.reg_load(reg, idx_sb[0:1, 0:1])
                            xs_["val"] = e.snap(reg, min_val=0, max_val=gsz - 1)
                        return e.dma_start(out=sap, in_=dview(recvbs[i].ap()[bass.ds(xs_["val"] * 128, 128), o:o + n], a))
                    n_ += 1
                    P.add("sp", fn, reads=["recvb"], writes=ks, dma_key=key)
            P.add("dve", lambda e: e.tensor_scalar(carry[:], carry[:], kbt[:, 4 * n_real: 4 * n_real + 1], None, ALU.mult),
                  reads=["carry"], writes=["carry"])

        seq = 0
        load_x_dma(n_norm, (0, 1, 2, 3), "act")
        for t in range(n_halo):
            st["tile"] = seq
            seq += 1
            last_halo = (t == n_halo - 1)
            load_x_transposes()
            rms_norm(0)
            ffn("g1", "u1", "d1", on_chunk=lambda c, t=t: x1_store(t, c))
            rms_norm(16)
            grp = (0, 1, 2) if last_halo else (2,)
            for hh in range(4):
                kvq(t, hh, False, 0, bank, hh == 0, groups=grp)
            if last_halo:
                conv_branch(True)
                roll_g1_only()
            nxt = n_norm + t + 1 if t + 1 < n_halo else (0 if n_norm > 0 else None)
            if nxt is not None:
                load_x_dma(nxt, (2, 3), "sp")
                load_x_dma(nxt, (0, 1), "act")
            if t == 0:
                for g in range(2):
                    for c in range(4):
                        ensure_cast("in", [CK + 4 * g + c], NCH)
                        ensure_cast("in", [CV + 4 * g + c], NCH)
                for c in range(4):
                    for g in range(3):
                        ensure_cast("in", [CQ + 4 * g + c], NCH)
                for c in range(NCH):
                    ensure_cast("in", [CU + c], NCH)
                    ensure_cast("in", [CGC + c], NCH)
                    ensure_cast("in", [CGB + c], NCH)
                for c in range(NCH):
                    ensure_cast("ao", [c], 4)
                    ensure_cast("co", [c], NCH)
                    ensure_cast("in", [CGA + c], NCH)
                    ensure_cast("in", [CGV + c], NCH)
                for c in range(NCH):
                    ensure_cast("o", [c], NCH)
                for c in range(0, 16, 2):
                    ensure_cast("g2", [c, c + 1], NCH)
                    ensure_cast("u2", [c, c + 1], NCH)
        exchange_send()

        for rt in range(n_real):
            j = n_halo + rt
            st["tile"] = seq
            seq += 1
            own_ffn1 = rt < n_norm
            if own_ffn1:
                load_x_transposes()
                rms_norm(0)
                ffn("g1", "u1", "d1")
            elif rt == 0:
                x1_load(0)
            rms_norm(16)
            if rt == 0:
                exchange_recv()
            attention(j, rt)
            conv_branch(False)
            merge_and_out()
            rms_norm(32)
            nxt_own = (rt + 1 < n_norm)
            if nxt_own:
                ffn("g2", "u2", "d2", after_gateup=lambda rt=rt: load_x_dma(rt + 1, (0, 1), "act"))
            else:
                ffn("g2", "u2", "d2")
            nxt_x1 = (rt + 1 < n_real) and not nxt_own
            store_x(rt, x1_next=(rt + 1 - n_norm) if nxt_x1 else None)
            if nxt_own:
                load_x_dma(rt + 1, (2, 3), "sp")

        P.assign_ticks()
        dma_sems = {}
        for k in dma_keys:
            if k not in dma_sems:
                dma_sems[k] = es.enter_context(nc.semaphore("d_" + str(len(dma_sems))))
        for k in P.dma_counts:
            assert k in dma_sems, k
        with nc.Block() as block:
            @block.sync
            def _(e):
                P.emit_queue("sp", e, eng_sems, dma_sems)

            @block.tensor
            def _(e):
                P.emit_queue("pe", e, eng_sems, dma_sems)

            @block.scalar
            def _(e):
                P.emit_queue("act", e, eng_sems, dma_sems)

            @block.vector
            def _(e):
                P.emit_queue("dve", e, eng_sems, dma_sems)

            @block.gpsimd
            def _(e):
                P.emit_queue("pool", e, eng_sems, dma_sems, final_dma_wait=True)
    return nc


def make_tabs():
    tabs = np.zeros((128, 576), np.float32)
    k = np.arange(128)[:, None]
    q = np.arange(256)[None, :]
    dist = q - k
    tabs[:, 0:256] = np.where((dist >= 0) & (dist <= 128), -dist, NEGM)
    for ph in range(5):
        p = np.arange(128)[:, None]
        qq = np.arange(32)[None, :]
        sl, klo = p // 32, p % 32
        dl = (ph - sl) % 5
        dist = 32 * dl + qq - klo
        tabs[:, 256 + 64 * ph: 256 + 64 * ph + 32] = np.where((dist >= 0) & (dist <= 128), -dist, NEGM)
        klo = np.arange(32)[:, None]
        dl = (ph - 4) % 5
        dist = 32 * dl + qq - klo
        tabs[0:32, 256 + 64 * ph + 32: 256 + 64 * ph + 64] = np.where((dist >= 0) & (dist <= 128), -dist, NEGM)
    return tabs


def make_kb(n_real, n_halo, seq_start):
    kb = np.zeros((128, 4 * n_real + 4), np.float32)
    kb[:, 4 * n_real] = 0.0 if seq_start else 1.0
    if not seq_start:
        return kb
    NEG = -30000.0
    for jr in range(n_real):
        j = n_halo + jr
        if j - 1 < n_halo:
            kb[:, 4 * jr + 0] = NEG
            kb[:, 4 * jr + 1] = NEG
        ph = j % 5
        p = np.arange(128)
        dl = (ph - p // 32) % 5
        kb[:, 4 * jr + 2] = np.where((dl > 0) & (j - dl < n_halo), NEG, 0.0)
        dl = (ph - 4) % 5
        if dl > 0 and j - dl < n_halo:
            kb[0:32, 4 * jr + 3] = NEG
    return kb


def make_vecs(ffn1_norm, mix_norm, ffn2_norm, q_norm, k_norm, conv_w):
    v = np.zeros((128, 120), np.float32)
    v[:, 0:16] = ffn1_norm.reshape(16, 128).T
    v[:, 16:32] = mix_norm.reshape(16, 128).T
    v[:, 32:48] = ffn2_norm.reshape(16, 128).T
    v[:, 48:60] = q_norm.reshape(12, 128).T
    v[:, 60:72] = k_norm.reshape(12, 128).T
    v[:, 72:120] = conv_w.reshape(3, 16, 128).transpose(2, 1, 0).reshape(128, 48)
    return v


_NC_CACHE = {}


def kernel(x, ffn1_norm, ffn1_w_gate, ffn1_w_up, ffn1_w_down, mix_norm, w_in,
           q_norm, k_norm, conv_w, w_attn_out, w_conv_out, w_o,
           ffn2_norm, ffn2_w_gate, ffn2_w_up, ffn2_w_down):
    x = np.asarray(x, np.float32)
    n_real, n_halo = TOK_CORE // T, HALO // T
    if "nc" not in _NC_CACHE:
        _NC_CACHE["nc"] = build_program(n_real, N_CORES)
    nc = _NC_CACHE["nc"]
    f = lambda a: np.ascontiguousarray(np.asarray(a, np.float32)[0])
    shared = {
        "ffn1_w_gate": f(ffn1_w_gate), "ffn1_w_up": f(ffn1_w_up), "ffn1_w_down": f(ffn1_w_down),
        "ffn2_w_gate": f(ffn2_w_gate), "ffn2_w_up": f(ffn2_w_up), "ffn2_w_down": f(ffn2_w_down),
        "w_in": f(w_in), "w_attn_out": f(w_attn_out), "w_conv_out": f(w_conv_out), "w_o": f(w_o),
        "vecs": make_vecs(f(ffn1_norm), f(mix_norm), f(ffn2_norm), f(q_norm), f(k_norm), f(conv_w)),
        "tabs": make_tabs(),
        "ident": np.eye(128, dtype=np.float32),
    }
    in_maps = []
    segs = SEQ // TOK_CORE
    for core in range(N_CORES):
        b, sgm = core // segs, core % segs
        s0 = sgm * TOK_CORE
        m = dict(shared)
        m["x"] = np.ascontiguousarray(x[b, s0:s0 + TOK_CORE])
        m["kb"] = make_kb(n_real, n_halo, sgm == 0)
        m["idx"] = np.array([[(sgm - 1) % segs]], np.int32)
        in_maps.append(m)
    res = run_bass_kernel_spmd(nc, in_maps, core_ids=list(range(N_CORES)))
    out = np.zeros((BATCH, SEQ, D_MODEL), np.float32)
    for core in range(N_CORES):
        b, sgm = core // segs, core % segs
        out[b, sgm * TOK_CORE:(sgm + 1) * TOK_CORE] = res.results[core]["out"]
    return out
```

```python
import numpy as np
from contextlib import ExitStack
import concourse.bass as bass
import concourse.mybir as mybir
from concourse.bass_utils import run_bass_kernel_spmd

F32 = mybir.dt.float32
BF16 = mybir.dt.bfloat16
AF = mybir.ActivationFunctionType
ALU = mybir.AluOpType

D_MODEL = 2048
D_FF = 5632
NCH = D_MODEL // 128
NFF = D_FF // 128
NFH = NFF // 2
T = 512
SEQ = 16384
BATCH = 2
N_CORES = 8
TOK_CORE = BATCH * SEQ // N_CORES
HALO = 2048
RMS_EPS = 1e-6
NEGM = -1.0e6
DIL = (1, 4, 16)
SLOPES = [2.0 ** (-8.0 * (i + 1) / 12) for i in range(12)]
W_IN_COLS = 14848
CQ, CK, CV, CU, CGB, CGC, CGA, CGV = 0, 12, 24, 36, 52, 68, 84, 100

QUEUES = ("pe", "act", "dve", "pool", "sp")


class Op:
    __slots__ = ("q", "fn", "deps", "signal", "tick", "dma_key", "dma_val", "has_dependents")

    def __init__(self, q, fn):
        self.q = q
        self.fn = fn
        self.deps = []
        self.signal = False
        self.tick = 0
        self.dma_key = None
        self.dma_val = 0
        self.has_dependents = False


class Prog:
    def __init__(self, same_engine_sync=True):
        self.q = {k: [] for k in QUEUES}
        self.lastw = {}
        self.readers = {}
        self.dma_counts = {}
        self.same_engine_sync = same_engine_sync
        self.pending_bar = {}

    def barrier(self, key):
        for q in QUEUES:
            self.pending_bar[q] = key

    def add(self, q, fn, reads=(), writes=(), dma_key=None):
        op = Op(q, fn)
        if q in self.pending_bar:
            reads = list(reads) + [self.pending_bar.pop(q)]
        deps = {}
        for r in reads:
            w = self.lastw.get(r)
            if w is not None:
                deps[id(w)] = w
        for w_ in writes:
            w = self.lastw.get(w_)
            if w is not None:
                deps[id(w)] = w
            rd = self.readers.get(w_)
            if rd:
                for o in rd[0].values():
                    deps[id(o)] = o
                for o in rd[1]:
                    deps[id(o)] = o
        for d in deps.values():
            if d.dma_key is None and d.q == q:
                if q == "pe" or not self.same_engine_sync:
                    continue
            op.deps.append(d)
            d.has_dependents = True
        for r in reads:
            rd = self.readers.get(r)
            if rd is None:
                rd = ({}, [])
                self.readers[r] = rd
            if dma_key is not None:
                rd[1].append(op)
            else:
                rd[0][q] = op
        for w_ in writes:
            self.lastw[w_] = op
            self.readers[w_] = ({}, [])
        if dma_key is not None:
            op.dma_key = dma_key
            c = self.dma_counts.get(dma_key, 0) + 16
            self.dma_counts[dma_key] = c
            op.dma_val = c
        self.q[q].append(op)
        return op

    def assign_ticks(self):
        for qname in QUEUES:
            t = 0
            for op in self.q[qname]:
                if op.dma_key is None and op.has_dependents:
                    t += 1
                    op.signal = True
                    op.tick = t

    def emit_queue(self, qname, eng, eng_sems, dma_sems, final_dma_wait=False):
        known = {}
        for op in self.q[qname]:
            need = {}
            for d in op.deps:
                if d.dma_key is not None:
                    key = ("dma", d.dma_key)
                    val = d.dma_val
                    sem = dma_sems[d.dma_key]
                else:
                    key = ("q", d.q)
                    val = d.tick
                    sem = eng_sems[d.q]
                if known.get(key, 0) >= val:
                    continue
                if key not in need or need[key][1] < val:
                    need[key] = (sem, val)
            for key, (sem, val) in need.items():
                known[key] = val
                eng.wait_ge(sem, val)
            ins = op.fn(eng)
            if op.dma_key is not None:
                ins.then_inc(dma_sems[op.dma_key], 16)
            elif op.signal:
                ins.then_inc(eng_sems[qname], 1)
        if final_dma_wait:
            for key, cnt in self.dma_counts.items():
                if known.get(("dma", key), 0) < cnt:
                    eng.wait_ge(dma_sems[key], cnt)


def bc(ap, n):
    dims = list(ap.ap)
    return bass.AP(ap.tensor, ap.offset, [dims[0], (0, n)] + dims[1:])


def build_program(n_tiles=12, n_halo=4, same_engine_sync=True):
    n_real = n_tiles - n_halo
    nc = bass.Bass("TRN2", target_bir_lowering=False)
    P = Prog(same_engine_sync=same_engine_sync)

    def din(name, shape, dt=F32):
        return nc.dram_tensor(name, list(shape), dt, kind="ExternalInput").ap()

    x_d = din("x", [n_tiles * T, D_MODEL])
    wsrc = {
        "g1": din("ffn1_w_gate", [D_MODEL, D_FF]), "u1": din("ffn1_w_up", [D_MODEL, D_FF]),
        "d1": din("ffn1_w_down", [D_FF, D_MODEL]),
        "g2": din("ffn2_w_gate", [D_MODEL, D_FF]), "u2": din("ffn2_w_up", [D_MODEL, D_FF]),
        "d2": din("ffn2_w_down", [D_FF, D_MODEL]),
        "in": din("w_in", [D_MODEL, W_IN_COLS]),
        "ao": din("w_attn_out", [512, D_MODEL]),
        "co": din("w_conv_out", [D_MODEL, D_MODEL]),
        "o": din("w_o", [D_MODEL, D_MODEL]),
    }
    vecs_d = din("vecs", [128, 120])
    tabs_d = din("tabs", [128, 576])
    kb_d = din("kb", [128, 4 * max(n_real, 1)])
    ident_d = din("ident", [128, 128])
    out_d = nc.dram_tensor("out", [max(n_real, 1) * T, D_MODEL], F32, kind="ExternalOutput").ap()

    def dscr(name, nchunks, nk):
        return nc.dram_tensor(name, [nchunks, 128, nk * 128], BF16, kind="Internal").ap()

    wbf = {
        "g1": dscr("wbf_g1", NFF, NCH), "u1": dscr("wbf_u1", NFF, NCH),
        "d1": dscr("wbf_d1", NCH * 2, NFH),
        "g2": dscr("wbf_g2", NFF, NCH), "u2": dscr("wbf_u2", NFF, NCH),
        "d2": dscr("wbf_d2", NCH * 2, NFH),
        "in": dscr("wbf_in", 116, NCH), "ao": dscr("wbf_ao", NCH, 4),
        "co": dscr("wbf_co", NCH, NCH), "o": dscr("wbf_o", NCH, NCH),
    }

    es = ExitStack()
    with es:
        def sb(name, shape, dt):
            return es.enter_context(nc.sbuf_tensor("sb_" + name, list(shape), dt))

        xT = sb("xT", [128, NCH * T], F32)
        hT = sb("hT", [128, NCH * T], BF16)
        big = sb("big", [128, 32 * T], BF16)
        bigf = big.bitcast(F32)
        attnT = sb("attnT", [128, 4 * T], BF16)
        K1 = sb("K1", [128, 4 * 640], BF16)
        V1 = sb("V1", [128, 4 * 640], BF16)
        K2 = sb("K2", [128, 4 * 4 * 256], BF16)
        V2 = sb("V2", [128, 4 * 4 * 256], BF16)
        K3 = sb("K3", [128, 4 * 16 * 160], BF16)
        V3 = sb("V3", [128, 4 * 16 * 160], BF16)
        NWU = 7
        WU = 2048
        wbuf = sb("wbuf", [128, NWU * WU], BF16)
        NSCR = 8
        SCRW = 520
        scr = sb("scr", [128, NSCR * SCRW], F32)
        scrb = scr.bitcast(BF16)
        ident_f = sb("ident_f", [128, 128], F32)
        ident_b = sb("ident_b", [128, 128], BF16)
        ones_b = sb("ones_b", [128, 128], BF16)
        zeros_b = sb("zeros_b", [128, 128], BF16)
        vecs = sb("vecs", [128, 120], F32)
        tabs = sb("tabs", [128, 576], F32)
        kbt = sb("kbt", [128, 4 * max(n_real, 1)], F32)
        carry = sb("carry", [128, NCH * 2], F32)
        cst = sb("cst", [128, 4], F32)

        psb = [es.enter_context(nc.psum_tensor("ps%d" % i, [128, 512], F32)) for i in range(8)]
        psb_b = [p.bitcast(BF16) for p in psb]

        eng_sems = {q: es.enter_context(nc.semaphore("s_" + q)) for q in ("pe", "act", "dve", "pool")}
        dma_keys = []

        st = {"bank": 0, "scr": 0, "wu": 0, "abank": 0, "tile": 0}

        def bank(pool8=True):
            if pool8:
                b = st["bank"] % 8
                st["bank"] += 1
            else:
                b = st["abank"] % 4
                st["abank"] += 1
            return b

        def newscr():
            s = st["scr"] % NSCR
            st["scr"] += 1
            return s

        def scr_f(s, n=T, off=0):
            return scr[:, s * SCRW + off: s * SCRW + off + n]

        def scr_b(s, n=T, off=0):
            return scrb[:, s * 2 * SCRW + off: s * 2 * SCRW + off + n]

        def big_b(g0, n):
            return big[:, g0 * T: g0 * T + n]

        def gran(g0, nbytes):
            return [("big", g0 + i) for i in range((nbytes + 1023) // 1024)]

        def xc(c):
            return xT[:, c * T:(c + 1) * T]

        def hc(c):
            return hT[:, c * T:(c + 1) * T]

        def vcol(i):
            return vecs[:, i:i + 1]

        def wload(wname, chunk0, nchunks, nk):
            per = nk * 128
            units_per = (per + WU - 1) // WU
            nun = units_per * nchunks
            u0 = st["wu"] % NWU
            if u0 + nun > NWU:
                u0 = 0
            st["wu"] = u0 + nun
            keys = [("w", u0 + i) for i in range(nun)]
            dkey = ("w", u0)
            if dkey not in dma_keys:
                dma_keys.append(dkey)
            src = wbf[wname][chunk0:chunk0 + nchunks, :, :].rearrange("c p e -> p c e")
            if units_per * WU == per:
                dst = wbuf[:, u0 * WU:(u0 + nun) * WU].rearrange("p (c e) -> p c e", c=nchunks)
            else:
                dst = wbuf[:, u0 * WU:(u0 + nun) * WU].rearrange("p (c e) -> p c e", c=nchunks)[:, :, 0:per]
            chunks = list(range(chunk0, chunk0 + nchunks))
            if st["tile"] == 0 and all((wname, ch) not in cast_done for ch in chunks):
                skeys = []
                for i, ch in enumerate(chunks):
                    base = (u0 + i * units_per) * WU
                    for pi, (psrc, _pdst, k) in enumerate(cast_pieces(wname, ch, nk)):
                        off = base + (11 * 128 * pi if nk == NFH else 0)
                        lv = wbuf[:, off: off + k * 128].rearrange("p (k m) -> p k m", k=k)
                        uk = u0 + i * units_per + pi
                        dk = ("wd", uk)
                        if dk not in dma_keys:
                            dma_keys.append(dk)
                        P.add("pool", lambda e, lv=lv, psrc=psrc: e.dma_start(out=lv, in_=psrc), writes=[("w", uk)], dma_key=dk)
                    cast_done[(wname, ch)] = [("wbf", wname, ch)]
                    skeys.append(("wbf", wname, ch))
                sk = ("ws", u0)
                if sk not in dma_keys:
                    dma_keys.append(sk)
                P.add("sp", lambda e, dst=dst, src=src: e.dma_start(out=src, in_=dst), reads=keys, writes=skeys, dma_key=sk)
            else:
                ckeys = ensure_cast(wname, chunks, nk)
                P.add("sp", lambda e, dst=dst, src=src: e.dma_start(out=dst, in_=src), reads=ckeys, writes=keys, dma_key=dkey)
            aps = []
            for i in range(nchunks):
                base = (u0 + i * units_per) * WU
                aps.append(wbuf[:, base:base + per].rearrange("p (k m) -> p k m", k=nk))
            return aps, keys

        def mm_group(pb, wap, nk, rhs_fn, reads, col0=0, ncol=T):
            def fn(e, pb=pb, wap=wap, nk=nk, rhs_fn=rhs_fn):
                last = None
                for k in range(nk):
                    last = e.matmul(psb[pb][:, col0:col0 + ncol], wap[:, k, :], rhs_fn(k),
                                    start=(k == 0), stop=(k == nk - 1))
                return last
            P.add("pe", fn, reads=reads, writes=[("ps", pb)])

        def mm_multi(outs, nk, rhs_fn, rhs_key_fn, wkeys):
            for k in range(nk):
                def fn(e, k=k):
                    last = None
                    for pb, wap in outs:
                        last = e.matmul(psb[pb][:, :], wap[:, k, :], rhs_fn(k), start=(k == 0), stop=(k == nk - 1))
                    return last
                P.add("pe", fn, reads=[rhs_key_fn(k)] + wkeys, writes=[("ps", pb) for pb, _ in outs])

        def cload(dst, src, key):
            dma_keys.append(key)
            P.add("sp", lambda e: e.dma_start(out=dst, in_=src), writes=[key], dma_key=key)

        cload(ident_f[:], ident_d[:, :], "c_ident")
        cload(vecs[:], vecs_d[:, :], "c_vecs")
        cload(tabs[:], tabs_d[:, :], "c_tabs")
        cload(kbt[:], kb_d[:, :], "c_kb")
        P.add("pool", lambda e: e.tensor_copy(ident_b[:], ident_f[:]), reads=["c_ident"], writes=["ident_b"])
        P.add("pool", lambda e: e.memset(ones_b[:], 1.0), writes=["ones_b"])
        P.add("pool", lambda e: e.memset(zeros_b[:], 0.0), writes=["zeros_b"])
        P.add("pool", lambda e: e.memset(cst[:, 0:1], RMS_EPS), writes=["cst0"])
        P.add("pool", lambda e: e.memset(cst[:, 1:2], 128.0 * RMS_EPS), writes=["cst1"])
        P.add("pool", lambda e: e.memset(cst[:, 2:3], 0.0), writes=["cst2"])
        P.add("pool", lambda e: e.memset(carry[:], 0.0), writes=["carry"])
        for ci_, t_ in enumerate((K1, V1, K2, V2, K3, V3)):
            P.add("pool", lambda e, t_=t_: e.memset(t_[:], 0.0), writes=["cache%d" % ci_])
        CONST_READS = ["c_ident", "c_vecs", "c_tabs", "c_kb", "ident_b", "ones_b", "zeros_b", "cst0", "cst1", "cst2"]

        NLAND = 2
        land = sb("land", [128, NLAND * 2048], BF16)
        for k_ in range(NLAND):
            dma_keys.append(("ci", k_))
            dma_keys.append(("co", k_))
        cast_done = {}
        cstate = {"n": 0}
        BAR = CONST_READS + ["cache%d" % i for i in range(6)] + ["carry"]
        P.add("pool", lambda e: e.memset(cst[:, 3:4], 0.0), reads=BAR, writes=["bar"])
        P.barrier("bar")

        def cast_pieces(wname, chunk, nk):
            if nk == NFH:
                c, half = chunk // 2, chunk % 2
                out = []
                for q0 in (0, 11):
                    kk0 = half * NFH + q0
                    src = wsrc[wname].rearrange("(k p) n -> p k n", p=128)[:, kk0:kk0 + 11, c * 128:(c + 1) * 128]
                    dst = wbf[wname][chunk, :, q0 * 128:(q0 + 11) * 128]
                    out.append((src, dst, 11))
                return out
            src = wsrc[wname].rearrange("(k p) n -> p k n", p=128)[:, :, chunk * 128:(chunk + 1) * 128]
            return [(src, wbf[wname][chunk, :, :], nk)]

        def ensure_cast(wname, chunks, nk):
            keys = []
            todo = []
            for ch in chunks:
                if (wname, ch) not in cast_done:
                    pcs = cast_pieces(wname, ch, nk)
                    ks = [("wbf", wname, ch, i) for i in range(len(pcs))]
                    cast_done[(wname, ch)] = ks
                    for (src, dst, k), key in zip(pcs, ks):
                        todo.append((src, dst, k, key))
                keys += cast_done[(wname, ch)]
            for g0 in range(0, len(todo), NLAND):
                grp = todo[g0:g0 + NLAND]
                slots = []
                for (src, dst, k, key) in grp:
                    s_ = cstate["n"] % NLAND
                    cstate["n"] += 1
                    slots.append(s_)
                    lv = land[:, s_ * 2048: s_ * 2048 + k * 128].rearrange("p (k m) -> p k m", k=k)
                    P.add("pool", lambda e, lv=lv, src=src: e.dma_start(out=lv, in_=src), writes=[("land", s_)], dma_key=("ci", s_))
                for (src, dst, k, key), s_ in zip(grp, slots):
                    lv2 = land[:, s_ * 2048: s_ * 2048 + k * 128]
                    P.add("pool", lambda e, lv2=lv2, dst=dst: e.dma_start(out=dst, in_=lv2), reads=[("land", s_)], writes=[key], dma_key=("co", s_))
            return keys

        def RB(q):
            return []

        hTf = hT.bitcast(F32)

        def xstage(b):
            if b < 2:
                return hTf[:, b * 2048:(b + 1) * 2048], [("h", 8 * b + i) for i in range(8)]
            return bigf[:, (b - 2) * 2048:(b - 1) * 2048], gran(8 * (b - 2), 8192)

        def load_x_dma(j, blocks, q):
            for b in blocks:
                key = ("xl", b)
                if key not in dma_keys:
                    dma_keys.append(key)
                dst, keys = xstage(b)
                P.add(q, lambda e, b=b, dst=dst: e.dma_start(out=dst, in_=x_d[j * T + b * 128: j * T + (b + 1) * 128, :]),
                      writes=keys, dma_key=key)

        def load_x_transposes():
            for half in range(2):
                for c in range(NCH):
                    pb = bank()
                    rk = []
                    for b in (2 * half, 2 * half + 1):
                        rk += xstage(b)[1]
                    def fn(e, c=c, pb=pb, half=half):
                        last = None
                        for bb in range(2):
                            src = xstage(2 * half + bb)[0]
                            last = e.transpose(psb[pb][:, bb * 128:(bb + 1) * 128], src[:, c * 128:(c + 1) * 128], ident_f[:])
                        return last
                    P.add("pe", fn, reads=rk, writes=[("ps", pb)])
                    dst = xT[:, c * T + half * 256: c * T + (half + 1) * 256]
                    if c % 2 == 0:
                        P.add("act", lambda e, dst=dst, pb=pb: e.copy(dst, psb[pb][:, 0:256]), reads=[("ps", pb)], writes=[("x", c)])
                    else:
                        P.add("dve", lambda e, dst=dst, pb=pb: e.tensor_copy(dst, psb[pb][:, 0:256]), reads=[("ps", pb)], writes=[("x", c)])

        def store_x(jr):
            stage = bigf
            for b in range(4):
                for g4 in range(4):
                    pb = bank()
                    def fn(e, b=b, g4=g4, pb=pb):
                        last = None
                        for cc in range(4):
                            c = g4 * 4 + cc
                            last = e.transpose(psb[pb][:, cc * 128:(cc + 1) * 128],
                                               xT[:, c * T + b * 128: c * T + (b + 1) * 128], ident_f[:])
                        return last
                    P.add("pe", fn, reads=[("x", g4 * 4 + cc) for cc in range(4)], writes=[("ps", pb)])
                    dst = stage[:, b * 2048 + g4 * 512: b * 2048 + (g4 + 1) * 512]
                    grs = gran(8 * b + 2 * g4, 2048)
                    if (b * 4 + g4) % 2 == 0:
                        P.add("act", lambda e, dst=dst, pb=pb: e.copy(dst, psb[pb][:, :]), reads=[("ps", pb)], writes=grs)
                    else:
                        P.add("dve", lambda e, dst=dst, pb=pb: e.tensor_copy(dst, psb[pb][:, :]), reads=[("ps", pb)], writes=grs)
                key = ("xs", b)
                if key not in dma_keys:
                    dma_keys.append(key)
                P.add("act", lambda e, b=b: e.dma_start(out=out_d[jr * T + b * 128: jr * T + (b + 1) * 128, :],
                                                          in_=stage[:, b * 2048:(b + 1) * 2048]),
                      reads=gran(8 * b, 8192), dma_key=key)

        def rms_norm(gain_base):
            pb = bank()
            sq = []
            for c in range(NCH):
                s = newscr()
                sq.append(s)
                P.add("act", lambda e, c=c, s=s: e.activation(scr_b(s), xc(c), AF.Square),
                      reads=[("x", c)], writes=[("scr", s)])
                P.add("pe", lambda e, c=c, s=s, pb=pb: e.matmul(psb[pb][:, :], ones_b[:], scr_b(s),
                                                                 start=(c == 0), stop=(c == NCH - 1)),
                      reads=[("scr", s)], writes=[("ps", pb)])
            s1 = newscr()
            P.add("act", lambda e: e.activation(scr_f(s1), psb[pb][:, :], AF.Ln, bias=cst[:, 0:1], scale=1.0 / D_MODEL),
                  reads=[("ps", pb)], writes=[("scr", s1)])
            P.add("act", lambda e: e.activation(scr_f(s1), scr_f(s1), AF.Exp, scale=-0.5), reads=[("scr", s1)], writes=[("scr", s1)])
            for c in range(NCH):
                P.add("dve", lambda e, c=c: e.scalar_tensor_tensor(hc(c), xc(c), vcol(gain_base + c), scr_f(s1), ALU.mult, ALU.mult),
                      reads=[("x", c), ("scr", s1)], writes=[("h", c)])

        def ffn(wg, wu, wd, after_gateup=None):
            HREADS = [("h", c) for c in range(NCH)]
            for half in range(2):
                for i in range(NFH // 2):
                    bi = half * NFH + 2 * i
                    gaps, gk = wload(wg, bi, 2, NCH)
                    uaps, uk = wload(wu, bi, 2, NCH)
                    first = (half == 0 and i == 0)
                    if first:
                        fb = [bank() for _ in range(4)]
                        mm_multi([(fb[0], gaps[0]), (fb[1], uaps[0]), (fb[2], gaps[1]), (fb[3], uaps[1])], NCH,
                                 lambda k: hc(k), lambda k: ("h", k), gk + uk)
                    for sub in range(2):
                        m = 2 * i + sub
                        if first:
                            pg, pu = fb[2 * sub], fb[2 * sub + 1]
                        else:
                            pg = bank()
                            mm_group(pg, gaps[sub], NCH, lambda k: hc(k), HREADS + gk)
                            pu = bank()
                            mm_group(pu, uaps[sub], NCH, lambda k: hc(k), HREADS + uk)
                        s = newscr()
                        P.add("act", lambda e, s=s, pg=pg: e.activation(scr_f(s), psb[pg][:, :], AF.Silu),
                              reads=[("ps", pg)], writes=[("scr", s)])
                        P.add("dve", lambda e, s=s, pu=pu, m=m: e.tensor_tensor(big_b(m, T), scr_f(s), psb[pu][:, :], ALU.mult),
                              reads=[("scr", s), ("ps", pu)], writes=[("big", m)])
                if half == 1 and after_gateup is not None:
                    after_gateup()
                for c in range(NCH):
                    daps, dk = wload(wd, c * 2 + half, 1, NFH)
                    pd = bank()
                    mm_group(pd, daps[0], NFH, lambda k: big_b(k, T), [("big", m) for m in range(NFH)] + dk)
                    P.add("dve", lambda e, c=c, pd=pd: e.scalar_tensor_tensor(xc(c), psb[pd][:, :], 0.5, xc(c), ALU.mult, ALU.add),
                          reads=[("ps", pd), ("x", c)], writes=[("x", c)])

        def kv_dst(cache, g, hh, j):
            if g == 0:
                return cache[:, hh * 640 + 128: hh * 640 + 640]
            if g == 1:
                s2 = j % 2
                v = cache[:, hh * 1024:(hh + 1) * 1024].rearrange("p (r s) -> p r s", r=4)
                return v[:, :, s2 * 128:(s2 + 1) * 128]
            s5 = j % 5
            v = cache[:, hh * 2560:(hh + 1) * 2560].rearrange("p (r s) -> p r s", r=16)
            return v[:, :, s5 * 32:(s5 + 1) * 32]

        def perm(ap, g):
            if g == 0:
                return ap
            return ap.rearrange("p (i r) -> p r i", r=DIL[g])

        CACHES = ((K1, V1), (K2, V2), (K3, V3))

        def head_post(pk, gain_col, is_q, dst, dst_keys, g, bankfn):
            s = newscr()
            P.add("act", lambda e: e.activation(scr_b(s), psb[pk][:, :], AF.Square), reads=[("ps", pk)], writes=[("scr", s)])
            pss = bankfn()
            P.add("pe", lambda e: e.matmul(psb[pss][:, :], ones_b[:], scr_b(s), start=True, stop=True),
                  reads=[("scr", s)], writes=[("ps", pss)])
            s2 = newscr()
            if is_q:
                P.add("act", lambda e: e.activation(scr_f(s2), psb[pss][:, :], AF.Ln, bias=cst[:, 1:2], scale=1.0),
                      reads=[("ps", pss)], writes=[("scr", s2)])
            else:
                P.add("act", lambda e: e.activation(scr_f(s2), psb[pss][:, :], AF.Ln, bias=cst[:, 0:1], scale=1.0 / 128.0),
                      reads=[("ps", pss)], writes=[("scr", s2)])
            P.add("act", lambda e: e.activation(scr_f(s2), scr_f(s2), AF.Exp, scale=-0.5), reads=[("scr", s2)], writes=[("scr", s2)])
            P.add("dve", lambda e: e.scalar_tensor_tensor(dst, perm(psb[pk][:, :], g), gain_col, perm(scr_f(s2), g), ALU.mult, ALU.mult),
                  reads=[("ps", pk), ("scr", s2)], writes=dst_keys)

        def kvq(j, hh, with_q, qset, bankfn, first, groups=(0, 1, 2)):
            HREADS = [("h", c) for c in range(NCH)]
            items = []
            for g in groups:
                aps, wk = wload("in", CK + 4 * g + hh, 1, NCH)
                items.append(("k", g, aps[0], wk))
            if with_q:
                for g in range(3):
                    aps, wk = wload("in", CQ + 4 * g + hh, 1, NCH)
                    items.append(("q", g, aps[0], wk))
            banks = []
            nfirst = 4 if (first and with_q) else (len(groups) if first else 0)
            if nfirst:
                fb = [bankfn() for _ in range(nfirst)]
                allk = []
                for it in items[:nfirst]:
                    allk += it[3]
                mm_multi([(fb[i], items[i][2]) for i in range(nfirst)], NCH, lambda k: hc(k), lambda k: ("h", k), allk)
            for idx, (kind, g, wap, wk) in enumerate(items):
                if idx < nfirst:
                    pk = fb[idx]
                else:
                    pk = bankfn()
                    mm_group(pk, wap, NCH, lambda k: hc(k), HREADS + wk)
                hd = 4 * g + hh
                if kind == "k":
                    head_post(pk, vcol(60 + hd), False, kv_dst(CACHES[g][0], g, hh, j), [("K", g, hh)], g, bankfn)
                else:
                    dst = big_b(qset * 3 + g, T)
                    if g > 0:
                        dst = dst.rearrange("p (r i) -> p r i", r=DIL[g])
                    head_post(pk, vcol(48 + hd), True, dst, [("big", qset * 3 + g)], g, bankfn)
            for g in groups:
                aps, wk = wload("in", CV + 4 * g + hh, 1, NCH)
                pv = bankfn()
                mm_group(pv, aps[0], NCH, lambda k: hc(k), HREADS + wk)
                P.add("act", lambda e, pv=pv, g=g: e.copy(kv_dst(CACHES[g][1], g, hh, j), perm(psb[pv][:, :], g)),
                      reads=[("ps", pv)], writes=[("V", g, hh)])

        def attention(j, jr):
            s2 = j % 2
            ph = j % 5
            VS1 = 12 * T
            VS2 = VS1 + 5 * 128
            VS3X = VS2 + 8 * 128
            VS3Y = VS3X + 16 * 128
            VSKEYS = [("big", i) for i in range(12, 24)]
            PB = 6 * T
            PG = 6
            def pbuf(i):
                return big[:, PB + i * T: PB + (i + 1) * T]
            def attn_head(hh):
                qs = hh % 2
                k1 = K1[:, hh * 640:(hh + 1) * 640]
                v1 = V1[:, hh * 640:(hh + 1) * 640]
                k2 = K2[:, hh * 1024:(hh + 1) * 1024].rearrange("p (r s) -> p r s", r=4)
                v2 = V2[:, hh * 1024:(hh + 1) * 1024].rearrange("p (r s) -> p r s", r=4)
                k3 = K3[:, hh * 2560:(hh + 1) * 2560].rearrange("p (r s) -> p r s", r=16)
                v3 = V3[:, hh * 2560:(hh + 1) * 2560].rearrange("p (r s) -> p r s", r=16)
                q1 = big_b(qs * 3 + 0, T)
                q2 = big_b(qs * 3 + 1, T).rearrange("p (r i) -> p r i", r=4)
                q3 = big_b(qs * 3 + 2, T).rearrange("p (r i) -> p r i", r=16)
                KR = [[("K", g, hh)] for g in range(3)]
                VR = [[("V", g, hh)] for g in range(3)]
                sb_ = [bank(False) for _ in range(6)]
                def s_g1(e, kind, pb):
                    last = None
                    for c in range(4):
                        kb0 = 128 * c if kind == 0 else 128 * (c + 1)
                        last = e.matmul(psb[pb][:, c * 128:(c + 1) * 128], k1[:, kb0:kb0 + 128], q1[:, c * 128:(c + 1) * 128],
                                        start=True, stop=True)
                    return last
                def s_g2(e, kind, pb):
                    last = None
                    so = (1 - s2) if kind == 0 else s2
                    for r in range(4):
                        last = e.matmul(psb[pb][:, r * 128:(r + 1) * 128], k2[:, r, so * 128:(so + 1) * 128], q2[:, r, :],
                                        start=True, stop=True)
                    return last
                def s_g3(e, kind, pb):
                    last = None
                    for r in range(16):
                        if kind == 0:
                            last = e.matmul(psb[pb][:, r * 32:(r + 1) * 32], k3[:, r, 0:128], q3[:, r, :], start=True, stop=True)
                        else:
                            last = e.matmul(psb[pb][0:32, r * 32:(r + 1) * 32], k3[:, r, 128:160], q3[:, r, :], start=True, stop=True)
                    return last
                sfn = [lambda e, pb=sb_[0]: s_g1(e, 0, pb), lambda e, pb=sb_[1]: s_g1(e, 1, pb),
                       lambda e, pb=sb_[2]: s_g2(e, 0, pb), lambda e, pb=sb_[3]: s_g2(e, 1, pb),
                       lambda e, pb=sb_[4]: s_g3(e, 0, pb), lambda e, pb=sb_[5]: s_g3(e, 1, pb)]
                sg = [0, 0, 1, 1, 2, 2]
                qg = [qs * 3, qs * 3, qs * 3 + 1, qs * 3 + 1, qs * 3 + 2, qs * 3 + 2]
                Dt = tabs
                tviews = [bc(Dt[:, 128:256], 4), bc(Dt[:, 0:128], 4), bc(Dt[:, 128:256], 4), bc(Dt[:, 0:128], 4),
                          bc(Dt[:, 256 + 64 * ph: 256 + 64 * ph + 32], 16), bc(Dt[0:32, 256 + 64 * ph + 32: 256 + 64 * ph + 64], 16)]
                shp = [(128, 4), (128, 4), (128, 4), (128, 4), (128, 16), (32, 16)]
                for i in range(6):
                    g = sg[i]
                    cval = SLOPES[4 * g + hh] * DIL[g]
                    P.add("pe", sfn[i], reads=KR[g] + [("big", qg[i])], writes=[("ps", sb_[i])])
                    s = newscr()
                    np_, nu = shp[i]
                    tmp = scr_f(s)[0:np_, :].rearrange("p (u n) -> p u n", u=nu)
                    sin = psb[sb_[i]][0:np_, :].rearrange("p (u n) -> p u n", u=nu)
                    P.add("dve", lambda e, tmp=tmp, sin=sin, tv=tviews[i], cval=cval: e.scalar_tensor_tensor(tmp, tv, cval, sin, ALU.mult, ALU.add),
                          reads=[("ps", sb_[i])], writes=[("scr", s)])
                    pdst = pbuf(i)
                    pkey = [("big", PG + i)]
                    if i == 0:
                        P.add("act", lambda e, s=s, pdst=pdst: e.activation(pdst[:, 0:128], scr_f(s)[:, 0:128], AF.Exp, bias=kbt[:, 4 * jr: 4 * jr + 1]),
                              reads=[("scr", s)], writes=pkey)
                        P.add("act", lambda e, s=s, pdst=pdst: e.activation(pdst[:, 128:512], scr_f(s)[:, 128:512], AF.Exp),
                              reads=[("scr", s)], writes=[("big", PG + i, "b")])
                    elif i == 2:
                        P.add("act", lambda e, s=s, pdst=pdst: e.activation(pdst[:, :], scr_f(s), AF.Exp, bias=kbt[:, 4 * jr + 1: 4 * jr + 2]),
                              reads=[("scr", s)], writes=pkey)
                    elif i == 4:
                        P.add("act", lambda e, s=s, pdst=pdst: e.activation(pdst[:, :], scr_f(s), AF.Exp, bias=kbt[:, 4 * jr + 2: 4 * jr + 3]),
                              reads=[("scr", s)], writes=pkey)
                    elif i == 5:
                        P.add("act", lambda e, s=s, pdst=pdst: e.activation(pdst[0:32, :], scr_f(s)[0:32, :], AF.Exp, bias=kbt[0:32, 4 * jr + 3: 4 * jr + 4]),
                              reads=[("scr", s)], writes=pkey)
                    else:
                        P.add("act", lambda e, s=s, pdst=pdst: e.activation(pdst[:, :], scr_f(s), AF.Exp),
                              reads=[("scr", s)], writes=pkey)
                def tr_batch(srcs, np_, dst_off, evq):
                    for b0 in range(0, len(srcs), 8):
                        grp = srcs[b0:b0 + 8]
                        pb = bank(False)
                        def fn(e, grp=grp, pb=pb):
                            last = None
                            for t_, sap in enumerate(grp):
                                last = e.transpose(psb_b[pb][0:np_, t_ * 128:(t_ + 1) * 128], sap, ident_b[:])
                            return last
                        P.add("pe", fn, reads=VR[0] + VR[1] + VR[2], writes=[("ps", pb)])
                        n = len(grp) * 128
                        dst = big[0:np_, dst_off + b0 * 128: dst_off + b0 * 128 + n]
                        if evq == "act":
                            P.add("act", lambda e, dst=dst, pb=pb, n=n: e.copy(dst, psb_b[pb][0:np_, 0:n]), reads=[("ps", pb)], writes=VSKEYS)
                        else:
                            P.add("dve", lambda e, dst=dst, pb=pb, n=n: e.tensor_copy(dst, psb_b[pb][0:np_, 0:n]), reads=[("ps", pb)], writes=VSKEYS)
                tr_batch([v1[:, 128 * b:128 * (b + 1)] for b in range(5)], 128, VS1, "act")
                tr_batch([v2[:, r, so * 128:(so + 1) * 128] for r in range(4) for so in range(2)], 128, VS2, "act")
                tr_batch([v3[:, r, 0:128] for r in range(16)], 128, VS3X, "act")
                tr_batch([v3[:, r, 128:160] for r in range(16)], 32, VS3Y, "act")
                return lambda: attn_head_b(hh)

            def attn_head_b(hh):
                VS1_, VS2_, VS3X_, VS3Y_ = VS1, VS2, VS3X, VS3Y
                pN = 4 + bank(False)
                pL = 4 + bank(False)
                P.add("pe", lambda e, pN=pN: e.matmul(psb[pN][:, :], zeros_b[:], hc(0), start=True, stop=False),
                      reads=[("h", 0)], writes=[("ps", pN)])
                P.add("pe", lambda e, pL=pL: e.matmul(psb[pL][:, :], zeros_b[:], hc(0), start=True, stop=False),
                      reads=[("h", 0)], writes=[("ps", pL)])
                pN_g2 = psb[pN][:, :].rearrange("p (i r) -> p r i", r=4)
                pL_g2 = psb[pL][:, :].rearrange("p (i r) -> p r i", r=4)
                pN_g3 = psb[pN][:, :].rearrange("p (i r) -> p r i", r=16)
                pL_g3 = psb[pL][:, :].rearrange("p (i r) -> p r i", r=16)
                def pv_fn(e, pN=pN, pL=pL, pN_g2=pN_g2, pL_g2=pL_g2, pN_g3=pN_g3, pL_g3=pL_g3):
                    vs1 = big[:, VS1:VS1 + 640].rearrange("p (b d) -> p b d", b=5)
                    vs2 = big[:, VS2:VS2 + 1024].rearrange("p (b d) -> p b d", b=8)
                    vs3x = big[:, VS3X:VS3X + 2048].rearrange("p (b d) -> p b d", b=16)
                    vs3y = big[0:32, VS3Y:VS3Y + 2048].rearrange("p (b d) -> p b d", b=16)
                    e.matmul(psb[pL][:, :], ones_b[:], pbuf(0)[:, :], start=False, stop=False)
                    e.matmul(psb[pL][:, :], ones_b[:], pbuf(1)[:, :], start=False, stop=False)
                    for kind in range(2):
                        e.matmul(pL_g2, ones_b[:], pbuf(2 + kind)[:, :].rearrange("p (r i) -> p r i", r=4), start=False, stop=False)
                    e.matmul(pL_g3, ones_b[:], pbuf(4)[:, :].rearrange("p (r i) -> p r i", r=16), start=False, stop=False)
                    e.matmul(pL_g3, ones_b[0:32, :], pbuf(5)[0:32, :].rearrange("p (r i) -> p r i", r=16), start=False, stop=True)
                    for c in range(4):
                        for kind in range(2):
                            pp = pbuf(kind)[:, c * 128:(c + 1) * 128]
                            vb = vs1[:, c + kind, :]
                            e.matmul(psb[pN][:, c * 128:(c + 1) * 128], vb, pp, start=False, stop=False)
                    for r in range(4):
                        for kind in range(2):
                            so = (1 - s2) if kind == 0 else s2
                            pp = pbuf(2 + kind)[:, r * 128:(r + 1) * 128]
                            vb = vs2[:, r * 2 + so, :]
                            e.matmul(pN_g2[:, r, :], vb, pp, start=False, stop=False)
                    last = None
                    for r in range(16):
                        pp = pbuf(4)[:, r * 32:(r + 1) * 32]
                        e.matmul(pN_g3[:, r, :], vs3x[:, r, :], pp, start=False, stop=False)
                        pp = pbuf(5)[0:32, r * 32:(r + 1) * 32]
                        last = e.matmul(pN_g3[:, r, :], vs3y[:, r, :], pp, start=False, stop=(r == 15))
                    return last
                P.add("pe", pv_fn, reads=VSKEYS + [("big", PG + i) for i in range(6)] + [("big", PG, "b")],
                      writes=[("ps", pN), ("ps", pL)])
                s = newscr()
                P.add("dve", lambda e, s=s, pL=pL: e.reciprocal(scr_f(s), psb[pL][:, :]), reads=[("ps", pL)], writes=[("scr", s)])
                P.add("dve", lambda e, s=s, pN=pN, hh=hh: e.tensor_tensor(attnT[:, hh * T:(hh + 1) * T], psb[pN][:, :], scr_f(s), ALU.mult),
                      reads=[("ps", pN), ("scr", s)], writes=[("attnT", hh)])
            kvq(j, 0, True, 0, bank, True)
            pend = None
            for hh in range(4):
                partb = attn_head(hh)
                if hh < 3:
                    kvq(j, hh + 1, True, (hh + 1) % 2, lambda: bank(False), False)
                partb()
            for hh in range(4):
                P.add("dve", lambda e, hh=hh: e.tensor_copy(K1[:, hh * 640: hh * 640 + 128], K1[:, hh * 640 + 512: hh * 640 + 640]),
                      reads=[("K", 0, hh)], writes=[("K", 0, hh)])
                P.add("dve", lambda e, hh=hh: e.tensor_copy(V1[:, hh * 640: hh * 640 + 128], V1[:, hh * 640 + 512: hh * 640 + 640]),
                      reads=[("V", 0, hh)], writes=[("V", 0, hh)])

        def roll_g1_only():
            for hh in range(4):
                P.add("dve", lambda e, hh=hh: e.tensor_copy(K1[:, hh * 640: hh * 640 + 128], K1[:, hh * 640 + 512: hh * 640 + 640]),
                      reads=[("K", 0, hh)] + RB("pool"), writes=[("K", 0, hh)])
                P.add("dve", lambda e, hh=hh: e.tensor_copy(V1[:, hh * 640: hh * 640 + 128], V1[:, hh * 640 + 512: hh * 640 + 640]),
                      reads=[("V", 0, hh)], writes=[("V", 0, hh)])

        def conv_branch(state_only):
            HREADS = [("h", c) for c in range(NCH)]
            for c in range(NCH):
                uaps, uk = wload("in", CU + c, 1, NCH)
                caps, ck = wload("in", CGC + c, 1, NCH)
                pu = bank()
                mm_group(pu, uaps[0], NCH, lambda k: hc(k), HREADS + uk)
                pc = bank()
                mm_group(pc, caps[0], NCH, lambda k: hc(k), HREADS + ck)
                if not state_only:
                    baps, bk = wload("in", CGB + c, 1, NCH)
                    pbk = bank()
                    mm_group(pbk, baps[0], NCH, lambda k: hc(k), HREADS + bk)
                su = newscr()
                P.add("act", lambda e, su=su, pu=pu: e.copy(scr_f(su), psb[pu][:, :]), reads=[("ps", pu)], writes=[("scr", su)])
                sz = newscr()
                P.add("dve", lambda e, sz=sz, c=c: e.tensor_copy(scr_f(sz, 2, 0), carry[:, 2 * c: 2 * c + 2]),
                      reads=["carry"], writes=[("scr", sz)])
                P.add("dve", lambda e, sz=sz, su=su, pc=pc: e.tensor_tensor(scr_f(sz, T, 2), psb[pc][:, :], scr_f(su), ALU.mult),
                      reads=[("ps", pc), ("scr", su), ("scr", sz)], writes=[("scr", sz)])
                P.add("dve", lambda e, sz=sz, c=c: e.tensor_copy(carry[:, 2 * c: 2 * c + 2], scr_f(sz, 2, T)),
                      reads=[("scr", sz)], writes=["carry"])
                if state_only:
                    continue
                sa = newscr()
                w0, w1, w2 = vcol(72 + 3 * c), vcol(72 + 3 * c + 1), vcol(72 + 3 * c + 2)
                P.add("dve", lambda e, sa=sa, sz=sz, w0=w0: e.tensor_scalar(scr_f(sa), scr_f(sz, T, 0), w0, None, ALU.mult),
                      reads=[("scr", sz)], writes=[("scr", sa)])
                P.add("dve", lambda e, sa=sa, sz=sz, w1=w1: e.scalar_tensor_tensor(scr_f(sa), scr_f(sz, T, 1), w1, scr_f(sa), ALU.mult, ALU.add),
                      reads=[("scr", sz), ("scr", sa)], writes=[("scr", sa)])
                P.add("dve", lambda e, sa=sa, sz=sz, w2=w2: e.scalar_tensor_tensor(scr_f(sa), scr_f(sz, T, 2), w2, scr_f(sa), ALU.mult, ALU.add),
                      reads=[("scr", sz), ("scr", sa)], writes=[("scr", sa)])
                P.add("dve", lambda e, sa=sa, pbk=pbk, c=c: e.tensor_tensor(big_b(c, T), psb[pbk][:, :], scr_f(sa), ALU.mult),
                      reads=[("ps", pbk), ("scr", sa)], writes=[("big", c)])

        def merge_and_out():
            HREADS = [("h", c) for c in range(NCH)]
            for c in range(NCH):
                aaps, ak = wload("ao", c, 1, 4)
                coaps, cok = wload("co", c, 1, NCH)
                gaaps, gak = wload("in", CGA + c, 1, NCH)
                gvaps, gvk = wload("in", CGV + c, 1, NCH)
                pA = bank()
                mm_group(pA, aaps[0], 4, lambda k: attnT[:, k * T:(k + 1) * T], [("attnT", k) for k in range(4)] + ak)
                pB = bank()
                mm_group(pB, coaps[0], NCH, lambda k: big_b(k, T), [("big", k) for k in range(NCH)] + cok)
                pGa = bank()
                mm_group(pGa, gaaps[0], NCH, lambda k: hc(k), HREADS + gak)
                pGv = bank()
                mm_group(pGv, gvaps[0], NCH, lambda k: hc(k), HREADS + gvk)
                s1, s2_ = newscr(), newscr()
                P.add("act", lambda e, s1=s1, pGa=pGa: e.activation(scr_f(s1), psb[pGa][:, :], AF.Sigmoid), reads=[("ps", pGa)], writes=[("scr", s1)])
                P.add("act", lambda e, s2_=s2_, pGv=pGv: e.activation(scr_f(s2_), psb[pGv][:, :], AF.Sigmoid), reads=[("ps", pGv)], writes=[("scr", s2_)])
                P.add("dve", lambda e, s1=s1, pA=pA: e.tensor_tensor(scr_f(s1), psb[pA][:, :], scr_f(s1), ALU.mult),
                      reads=[("ps", pA), ("scr", s1)], writes=[("scr", s1)])
                P.add("dve", lambda e, s2_=s2_, pB=pB: e.tensor_tensor(scr_f(s2_), psb[pB][:, :], scr_f(s2_), ALU.mult),
                      reads=[("ps", pB), ("scr", s2_)], writes=[("scr", s2_)])
                P.add("dve", lambda e, s1=s1, s2_=s2_, c=c: e.tensor_tensor(big_b(16 + c, T), scr_f(s1), scr_f(s2_), ALU.add),
                      reads=[("scr", s1), ("scr", s2_)], writes=[("big", 16 + c)])
            for c in range(NCH):
                oaps, ok = wload("o", c, 1, NCH)
                po = bank()
                mm_group(po, oaps[0], NCH, lambda k: big_b(16 + k, T), [("big", 16 + k) for k in range(NCH)] + ok)
                P.add("dve", lambda e, c=c, po=po: e.tensor_tensor(xc(c), psb[po][:, :], xc(c), ALU.add),
                      reads=[("ps", po), ("x", c)], writes=[("x", c)])

        load_x_dma(0, (0, 1, 2, 3), "act")
        for j in range(n_tiles):
            halo = j < n_halo
            last_halo = (j == n_halo - 1)
            st["tile"] = j
            load_x_transposes()
            rms_norm(0)
            ffn("g1", "u1", "d1")
            rms_norm(16)
            if halo:
                grp = (0, 1, 2) if last_halo else (2,)
                for hh in range(4):
                    kvq(j, hh, False, 0, bank, hh == 0, groups=grp)
                if last_halo:
                    conv_branch(True)
                    roll_g1_only()
                if j + 1 < n_tiles:
                    load_x_dma(j + 1, (2, 3), "sp")
                    load_x_dma(j + 1, (0, 1), "act")
                continue
            jr = j - n_halo
            attention(j, jr)
            conv_branch(False)
            merge_and_out()
            rms_norm(32)
            if j + 1 < n_tiles:
                ffn("g2", "u2", "d2", after_gateup=lambda j=j: load_x_dma(j + 1, (0, 1), "act"))
            else:
                ffn("g2", "u2", "d2")
            store_x(jr)
            if j + 1 < n_tiles:
                load_x_dma(j + 1, (2, 3), "sp")

        P.assign_ticks()
        dma_sems = {}
        for k in dma_keys:
            if k not in dma_sems:
                dma_sems[k] = es.enter_context(nc.semaphore("d_" + str(len(dma_sems))))
        for k in P.dma_counts:
            assert k in dma_sems, k
        with nc.Block() as block:
            @block.sync
            def _(e):
                P.emit_queue("sp", e, eng_sems, dma_sems)

            @block.tensor
            def _(e):
                P.emit_queue("pe", e, eng_sems, dma_sems)

            @block.scalar
            def _(e):
                P.emit_queue("act", e, eng_sems, dma_sems)

            @block.vector
            def _(e):
                P.emit_queue("dve", e, eng_sems, dma_sems)

            @block.gpsimd
            def _(e):
                P.emit_queue("pool", e, eng_sems, dma_sems, final_dma_wait=True)
    return nc


def make_tabs():
    tabs = np.zeros((128, 576), np.float32)
    k = np.arange(128)[:, None]
    q = np.arange(256)[None, :]
    dist = q - k
    tabs[:, 0:256] = np.where((dist >= 0) & (dist <= 128), -dist, NEGM)
    for ph in range(5):
        p = np.arange(128)[:, None]
        qq = np.arange(32)[None, :]
        sl, klo = p // 32, p % 32
        dl = (ph - sl) % 5
        dist = 32 * dl + qq - klo
        tabs[:, 256 + 64 * ph: 256 + 64 * ph + 32] = np.where((dist >= 0) & (dist <= 128), -dist, NEGM)
        klo = np.arange(32)[:, None]
        dl = (ph - 4) % 5
        dist = 32 * dl + qq - klo
        tabs[0:32, 256 + 64 * ph + 32: 256 + 64 * ph + 64] = np.where((dist >= 0) & (dist <= 128), -dist, NEGM)
    return tabs


def make_kb(n_real, n_halo, seq_start):
    kb = np.zeros((128, 4 * max(n_real, 1)), np.float32)
    if not seq_start:
        return kb
    NEG = -30000.0
    for jr in range(n_real):
        j = n_halo + jr
        if j - 1 < n_halo:
            kb[:, 4 * jr + 0] = NEG
            kb[:, 4 * jr + 1] = NEG
        ph = j % 5
        p = np.arange(128)
        dl = (ph - p // 32) % 5
        kb[:, 4 * jr + 2] = np.where((dl > 0) & (j - dl < n_halo), NEG, 0.0)
        dl = (ph - 4) % 5
        if dl > 0 and j - dl < n_halo:
            kb[0:32, 4 * jr + 3] = NEG
    return kb


def make_vecs(ffn1_norm, mix_norm, ffn2_norm, q_norm, k_norm, conv_w):
    v = np.zeros((128, 120), np.float32)
    v[:, 0:16] = ffn1_norm.reshape(16, 128).T
    v[:, 16:32] = mix_norm.reshape(16, 128).T
    v[:, 32:48] = ffn2_norm.reshape(16, 128).T
    v[:, 48:60] = q_norm.reshape(12, 128).T
    v[:, 60:72] = k_norm.reshape(12, 128).T
    v[:, 72:120] = conv_w.reshape(3, 16, 128).transpose(2, 1, 0).reshape(128, 48)
    return v


_NC_CACHE = {}


def kernel(x, ffn1_norm, ffn1_w_gate, ffn1_w_up, ffn1_w_down, mix_norm, w_in,
           q_norm, k_norm, conv_w, w_attn_out, w_conv_out, w_o,
           ffn2_norm, ffn2_w_gate, ffn2_w_up, ffn2_w_down):
    x = np.asarray(x, np.float32)
    n_tiles, n_halo = (TOK_CORE + HALO) // T, HALO // T
    n_real = n_tiles - n_halo
    if "nc" not in _NC_CACHE:
        _NC_CACHE["nc"] = build_program(n_tiles, n_halo)
    nc = _NC_CACHE["nc"]
    f = lambda a: np.ascontiguousarray(np.asarray(a, np.float32)[0])
    shared = {
        "ffn1_w_gate": f(ffn1_w_gate), "ffn1_w_up": f(ffn1_w_up), "ffn1_w_down": f(ffn1_w_down),
        "ffn2_w_gate": f(ffn2_w_gate), "ffn2_w_up": f(ffn2_w_up), "ffn2_w_down": f(ffn2_w_down),
        "w_in": f(w_in), "w_attn_out": f(w_attn_out), "w_conv_out": f(w_conv_out), "w_o": f(w_o),
        "vecs": make_vecs(f(ffn1_norm), f(mix_norm), f(ffn2_norm), f(q_norm), f(k_norm), f(conv_w)),
        "tabs": make_tabs(),
        "ident": np.eye(128, dtype=np.float32),
    }
    in_maps = []
    segs = SEQ // TOK_CORE
    for core in range(N_CORES):
        b, sgm = core // segs, core % segs
        s0 = sgm * TOK_CORE
        xc_ = np.zeros((TOK_CORE + HALO, D_MODEL), np.float32)
        if sgm > 0:
            xc_[:] = x[b, s0 - HALO: s0 + TOK_CORE]
        else:
            xc_[HALO:] = x[b, 0:TOK_CORE]
        m = dict(shared)
        m["x"] = xc_
        m["kb"] = make_kb(n_real, n_halo, sgm == 0)
        in_maps.append(m)
    res = run_bass_kernel_spmd(nc, in_maps, core_ids=list(range(N_CORES)))
    out = np.zeros((BATCH, SEQ, D_MODEL), np.float32)
    for core in range(N_CORES):
        b, sgm = core // segs, core % segs
        out[b, sgm * TOK_CORE:(sgm + 1) * TOK_CORE] = res.results[core]["out"]
    return out
```
